# Optimizing a Trainium2 kernel written in Bass

```python
import jax, jax.numpy as jnp
from jax import lax
import numpy as np

D_MODEL = 4096
BATCH = 4
SEQ = 4096
DEPTH = 1

GRID_W = 64
CTX_LEN = 256
EXPAND = 2
MIX_W = EXPAND * D_MODEL
LRU_W = MIX_W // 2
POOL_W = MIX_W - LRU_W
LRU_HEADS = 16
LRU_HEAD_DIM = LRU_W // LRU_HEADS
CONV_W = 4
CONV_LEFT = 1
LRU_C = 8.0
POOL_WINDOWS = (2, 4, 8, 16)
POOL_GROUPS = len(POOL_WINDOWS)
POOL_GROUP_DIM = POOL_W // POOL_GROUPS
EPS = 1e-6

kernel_name = 'hybrid_rglru_pool_dit_block'


def rmsnorm(x, g):
    xf = x.astype(jnp.float32)
    y = xf * lax.rsqrt(jnp.mean(xf * xf, axis=-1, keepdims=True) + EPS) * g.astype(jnp.float32)
    return y.astype(x.dtype)


def modulate(h, shift, scale):
    return h * (1.0 + scale) + shift


def depthwise_conv(u, w, b):
    L = u.shape[1]
    up = jnp.pad(u, ((0, 0), (CONV_LEFT, CONV_W - 1 - CONV_LEFT), (0, 0)))
    y = b
    for k in range(CONV_W):
        y = y + up[:, k:k + L] * w[k]
    return y


def rglru_coeffs(u, lam, w_r, b_r, w_i, b_i):
    uf = u.astype(jnp.float32)
    Bn, L, _ = uf.shape
    uh = uf.reshape(Bn, L, LRU_HEADS, LRU_HEAD_DIM)
    r = jax.nn.sigmoid(jnp.einsum('blhi,hij->blhj', uh, w_r.astype(jnp.float32)).reshape(Bn, L, LRU_W) + b_r.astype(jnp.float32))
    i = jax.nn.sigmoid(jnp.einsum('blhi,hij->blhj', uh, w_i.astype(jnp.float32)).reshape(Bn, L, LRU_W) + b_i.astype(jnp.float32))
    log_a = (-LRU_C * jax.nn.softplus(-lam.astype(jnp.float32))) * r
    a = jnp.exp(log_a)
    b = jnp.sqrt(-jnp.expm1(2.0 * log_a)) * (i * uf)
    return a, b


def linear_scan(a, b, h0, reverse, return_seq):
    aT = jnp.swapaxes(a, 0, 1)
    bT = jnp.swapaxes(b, 0, 1)

    def step(h, ab):
        at, bt = ab
        h = at * h + bt
        return h, (h if return_seq else None)

    h_last, hs = lax.scan(step, h0, (aT, bT), reverse=reverse)
    return (jnp.swapaxes(hs, 0, 1) if return_seq else None), h_last


def rglru_branch(xa_lat, xa_ctx, conv_w, conv_b, lam, w_r, b_r, w_i, b_i, ctx_out):
    u_lat = depthwise_conv(xa_lat, conv_w, conv_b)
    u_ctx = depthwise_conv(xa_ctx, conv_w, conv_b)
    ys_lat, ys_ctx = [], []
    for d, reverse in enumerate((False, True)):
        a_c, b_c = rglru_coeffs(u_ctx, lam[d], w_r[d], b_r[d], w_i[d], b_i[d])
        h0 = jnp.zeros((u_ctx.shape[0], LRU_W), jnp.float32)
        hs_c, h_c = linear_scan(a_c, b_c, h0, reverse, ctx_out)
        a_l, b_l = rglru_coeffs(u_lat, lam[d], w_r[d], b_r[d], w_i[d], b_i[d])
        hs_l, _ = linear_scan(a_l, b_l, h_c, reverse, True)
        ys_lat.append(hs_l)
        ys_ctx.append(hs_c)
    y_lat = ys_lat[0] + ys_lat[1]
    y_ctx = (ys_ctx[0] + ys_ctx[1]) if ctx_out else None
    return y_lat, y_ctx


def centred_mean(v, w):
    L = v.shape[-2]
    left = w // 2
    right = w - 1 - left
    S = jnp.concatenate([jnp.zeros_like(v[..., :1, :]), lax.cumsum(v, axis=v.ndim - 2)], axis=-2)
    t = np.arange(L)
    lo = np.maximum(t - left, 0)
    hi = np.minimum(t + right, L - 1) + 1
    sums = jnp.take(S, jnp.asarray(hi), axis=-2) - jnp.take(S, jnp.asarray(lo), axis=-2)
    cnt = jnp.asarray((hi - lo).astype(np.float32))[:, None]
    return sums / cnt


def pool_branch(u, w_pool, b_pool, scale):
    uf = u.astype(jnp.float32)
    grp = uf.reshape(uf.shape[:-1] + (POOL_GROUPS, POOL_GROUP_DIM))
    outs = [centred_mean(grp[..., g, :], w) - grp[..., g, :] for g, w in enumerate(POOL_WINDOWS)]
    z = jnp.stack(outs, axis=-2)
    y = jnp.einsum('...gi,gij->...gj', z, w_pool.astype(jnp.float32)).reshape(uf.shape) + b_pool
    return y * scale


def layer(x, ctx, c, c_ctx, w_ada, b_ada, g_norm, w_in, conv_w, conv_b, lam,
          w_r, b_r, w_i, b_i, w_pool, b_pool, pool_scale, w_out, last):
    D = D_MODEL
    Bn, L, _ = x.shape
    rows = L // GRID_W
    ctx_out = not last
    mod = jax.nn.silu(c) @ w_ada + b_ada
    shift, scale, gate = jnp.split(mod, 3, axis=-1)
    h = modulate(rmsnorm(x, g_norm), shift[:, None], scale[:, None])
    proj = h @ w_in
    xa, xb, ga, gb = jnp.split(proj, [LRU_W, MIX_W, MIX_W + LRU_W], axis=-1)
    if ctx_out:
        mod_c = jax.nn.silu(c_ctx) @ w_ada + b_ada
        shift_c, scale_c, gate_c = jnp.split(mod_c, 3, axis=-1)
        hc = modulate(rmsnorm(ctx, g_norm), shift_c, scale_c)
        proj_c = hc @ w_in
    else:
        mod_c = jax.nn.silu(c_ctx) @ w_ada[:, :2 * D] + b_ada[:2 * D]
        shift_c, scale_c = jnp.split(mod_c, 2, axis=-1)
        hc = modulate(rmsnorm(ctx, g_norm), shift_c, scale_c)
        proj_c = hc @ w_in[:, :LRU_W]
    xa_c = proj_c[..., :LRU_W]
    ya, ya_c = rglru_branch(xa, xa_c, conv_w, conv_b, lam, w_r, b_r, w_i, b_i, ctx_out)
    yb = pool_branch(xb.reshape(Bn, rows, GRID_W, POOL_W), w_pool, b_pool, pool_scale).reshape(Bn, L, POOL_W)
    mixed = jnp.concatenate([ya * jax.nn.silu(ga.astype(jnp.float32)),
                             yb * jax.nn.silu(gb.astype(jnp.float32))], axis=-1).astype(x.dtype)
    x_new = (x + gate[:, None] * (mixed @ w_out)).astype(x.dtype)
    if ctx_out:
        _, xb_c, ga_c, gb_c = jnp.split(proj_c, [LRU_W, MIX_W, MIX_W + LRU_W], axis=-1)
        yb_c = pool_branch(xb_c, w_pool, b_pool, pool_scale)
        mixed_c = jnp.concatenate([ya_c * jax.nn.silu(ga_c.astype(jnp.float32)),
                                   yb_c * jax.nn.silu(gb_c.astype(jnp.float32))], axis=-1).astype(ctx.dtype)
        ctx_new = (ctx + gate_c * (mixed_c @ w_out)).astype(ctx.dtype)
    else:
        ctx_new = None
    return x_new, ctx_new


def setup_inputs(seed: int = 0) -> dict:
    key = jax.random.key(seed)
    ks = jax.random.split(key, 20)
    D = D_MODEL
    f32 = jnp.float32
    x = jax.random.normal(ks[0], (BATCH, SEQ, D), f32)
    c = jax.random.normal(ks[1], (BATCH, D), f32)
    ctx = jax.random.normal(ks[2], (BATCH, CTX_LEN, D), f32)
    c_ctx = jax.random.normal(ks[3], (D,), f32)
    w_ada = jax.random.normal(ks[4], (DEPTH, D, 3 * D), f32) * (0.5 * D ** -0.5)
    b_ada = 0.01 * jax.random.normal(ks[5], (DEPTH, 3 * D), f32)
    g_norm = 1.0 + 0.05 * jax.random.normal(ks[6], (DEPTH, D), f32)
    w_in = jax.random.normal(ks[7], (DEPTH, D, 2 * MIX_W), f32) * D ** -0.5
    conv_w = jax.random.normal(ks[8], (DEPTH, CONV_W, LRU_W), f32) * CONV_W ** -0.5
    conv_b = 0.01 * jax.random.normal(ks[9], (DEPTH, LRU_W), f32)
    a_target = jax.random.uniform(ks[10], (DEPTH, 2, LRU_W), f32, minval=0.9, maxval=0.999) ** (1.0 / LRU_C)
    lru_lambda = jnp.log(a_target) - jnp.log1p(-a_target)
    w_rgate = jax.random.normal(ks[11], (DEPTH, 2, LRU_HEADS, LRU_HEAD_DIM, LRU_HEAD_DIM), f32) * LRU_HEAD_DIM ** -0.5
    b_rgate = 0.01 * jax.random.normal(ks[12], (DEPTH, 2, LRU_W), f32)
    w_igate = jax.random.normal(ks[13], (DEPTH, 2, LRU_HEADS, LRU_HEAD_DIM, LRU_HEAD_DIM), f32) * LRU_HEAD_DIM ** -0.5
    b_igate = 0.01 * jax.random.normal(ks[14], (DEPTH, 2, LRU_W), f32)
    w_pool = jax.random.normal(ks[15], (DEPTH, POOL_GROUPS, POOL_GROUP_DIM, POOL_GROUP_DIM), f32) * POOL_GROUP_DIM ** -0.5
    b_pool = 0.01 * jax.random.normal(ks[16], (DEPTH, POOL_W), f32)
    pool_scale = 1.0 + 0.1 * jax.random.normal(ks[17], (DEPTH, POOL_W), f32)
    w_out = jax.random.normal(ks[18], (DEPTH, MIX_W, D), f32) * MIX_W ** -0.5
    g_final = 1.0 + 0.05 * jax.random.normal(ks[19], (D,), f32)
    return {'x': x, 'c': c, 'ctx': ctx, 'c_ctx': c_ctx, 'w_ada': w_ada, 'b_ada': b_ada,
            'g_norm': g_norm, 'w_in': w_in, 'conv_w': conv_w, 'conv_b': conv_b,
            'lru_lambda': lru_lambda, 'w_rgate': w_rgate, 'b_rgate': b_rgate,
            'w_igate': w_igate, 'b_igate': b_igate, 'w_pool': w_pool, 'b_pool': b_pool,
            'pool_scale': pool_scale, 'w_out': w_out, 'g_final': g_final}


def reference(x, c, ctx, c_ctx, w_ada, b_ada, g_norm, w_in, conv_w, conv_b, lru_lambda,
              w_rgate, b_rgate, w_igate, b_igate, w_pool, b_pool, pool_scale, w_out, g_final):
    for l in range(DEPTH):
        x, ctx = layer(x, ctx, c, c_ctx, w_ada[l], b_ada[l], g_norm[l], w_in[l], conv_w[l], conv_b[l],
                       lru_lambda[l], w_rgate[l], b_rgate[l], w_igate[l], b_igate[l],
                       w_pool[l], b_pool[l], pool_scale[l], w_out[l], last=(l == DEPTH - 1))
    return rmsnorm(x, g_final)
```

```python
import numpy as np
import concourse.bass as bass
import concourse.mybir as mybir
from concourse.bass_utils import run_bass_kernel_spmd

F32 = mybir.dt.float32
BF16 = mybir.dt.bfloat16
AF = mybir.ActivationFunctionType
ALU = mybir.AluOpType
EPS = 1e-6
NA = 50000
POOL_WINDOWS = (2, 4, 8, 16)
GRID_W = 64


class Cfg:
    def __init__(s, D=4096, W=4096, NOWN=2048, NCTX=256):
        s.D = D; s.W = W; s.NOWN = NOWN; s.NCTX = NCTX
        s.KC = D // 128; s.NCH = W // 128; s.NH = W // 256; s.PG = W // 4
        s.LT = 2 * NOWN; s.NOC = NCTX + NOWN; s.KC2 = 2 * W // 128


class Task:
    __slots__ = ("eng", "fn", "deps", "lane", "sem", "val", "needs_inc", "tag")


class Prog:
    ENG = ("pe", "act", "dve", "pool", "sp")
    CENG = ("pe", "act", "dve", "pool")

    def __init__(s, nc):
        s.nc = nc
        s.tasks = {e: [] for e in s.ENG}
        s.lane_cnt = {}
        s.lane_sem = {}
        s.last_dma = {}
        s.last_comp = {}
        s.engsem = {e: nc.alloc_semaphore(name="es_" + e) for e in s.CENG}
        s.stage = ""
        s.names = {}
        s.dead = False
        s.nobar = set()

    def _add(s, eng, fn, deps):
        t = Task()
        t.eng = eng; t.fn = fn; t.lane = None; t.sem = None; t.val = None; t.needs_inc = False
        dl = []
        for d in deps:
            if d is None:
                continue
            if isinstance(d, (list, tuple)):
                for x in d:
                    if x is None:
                        continue
                    if isinstance(x, (list, tuple)):
                        dl.extend(y for y in x if y is not None)
                    else:
                        dl.append(x)
            else:
                dl.append(d)
        t.deps = dl
        t.tag = s.stage
        if not s.dead:
            s.tasks[eng].append(t)
        return t

    def op(s, eng, fn, deps=()):
        t = s._add(eng, fn, deps)
        if fn is not None and not s.dead:
            s.last_comp[eng] = t
        return t

    def dma(s, q, lane, fn, deps=()):
        t = s._add(q, fn, deps)
        t.lane = lane
        if s.dead:
            return t
        if lane not in s.lane_sem:
            s.lane_sem[lane] = s.nc.alloc_semaphore(name="ls_" + lane)
            s.lane_cnt[lane] = 0
        s.lane_cnt[lane] += 1
        t.sem = s.lane_sem[lane]
        t.val = 16 * s.lane_cnt[lane]
        s.last_dma[lane] = t
        return t

    def checkpoint(s, name, stop):
        if stop == name:
            s.dead = True

    def barrier(s, final=False):
        if final:
            s.dead = False
        deps = [s.last_comp[e] for e in s.CENG if e in s.last_comp] + [v for k, v in s.last_dma.items() if final or k not in s.nobar]
        for e in s.ENG:
            s.op(e, None, deps)

    def finalize(s):
        for e in s.ENG:
            for t in s.tasks[e]:
                for d in t.deps:
                    if d.lane is None:
                        d.needs_inc = True
        for e in s.CENG:
            n = 0
            for t in s.tasks[e]:
                if t.lane is None and t.fn is not None and t.needs_inc:
                    n += 1
                    t.sem = s.engsem[e]
                    t.val = n

    def emit(s, name, e):
        waited = {}
        for t in s.tasks[name]:
            need = {}
            for d in t.deps:
                k = id(d.sem)
                if k not in need or need[k][1] < d.val:
                    need[k] = (d.sem, d.val)
            for k, (sm, v) in need.items():
                if waited.get(k, 0) < v:
                    e.wait_ge(sm, v)
                    waited[k] = v
            if t.fn is None:
                continue
            ins = t.fn(e)
            try:
                s.names[str(ins.ins.name)] = (name, t.tag)
            except Exception:
                pass
            if t.lane is not None:
                ins.then_inc(t.sem, 16)
            elif t.needs_inc:
                ins.then_inc(t.sem, 1)


def mm(out, lhsT, rhs, start, stop):
    return lambda e: e.matmul(out, lhsT, rhs, start=start, stop=stop)


def tr(out, in_, ident):
    return lambda e: e.transpose(out, in_, ident)


def act(out, in_, func, bias=None, scale=None, accum=None):
    def f(e):
        kw = {}
        if bias is not None:
            kw["bias"] = bias
        if scale is not None:
            kw["scale"] = scale
        if accum is not None:
            kw["accum_out"] = accum
        return e.activation(out, in_, func, **kw)
    return f


def dmaf(out, in_):
    return lambda e: e.dma_start(out=out, in_=in_)


def tt(out, a, b, op):
    return lambda e: e.tensor_tensor(out, a, b, op)


def ts(out, a, s1, s2, op0, op1):
    return lambda e: e.tensor_scalar(out, a, s1, s2, op0, op1)


def ts1(out, a, s1, op):
    return lambda e: e.tensor_single_scalar(out, a, s1, op)


def stt(out, in0, scalar, in1, op0, op1):
    return lambda e: e.scalar_tensor_tensor(out, in0, scalar, in1, op0, op1)


def cp(out, in_):
    return lambda e: e.tensor_copy(out, in_)


def scan(out, d0, d1, init):
    return lambda e: e.tensor_tensor_scan(out, d0, d1, init, ALU.mult, ALU.add)


def mset(ap, v):
    return lambda e: e.memset(ap, v)


def recip(out, in_):
    return lambda e: e.reciprocal(out, in_)


class Arena:
    def __init__(s, ar):
        s.ar = ar; s.off = 0

    def f32(s, n):
        s.off = (s.off + 15) // 16 * 16
        v = s.ar[:, s.off:s.off + n]
        s.off += n
        assert s.off <= NA, ("arena overflow", s.off)
        return v

    def bf16(s, n):
        w = (n + 1) // 2
        s.off = (s.off + 15) // 16 * 16
        v = s.ar[:, s.off:s.off + w].bitcast(BF16)
        s.off += w
        assert s.off <= NA, ("arena overflow", s.off)
        return v


def r3(ap, b):
    return ap.rearrange("p (a b) -> p a b", b=b)


def blocks_of(n, bs=512):
    out = []
    t0 = 0
    while t0 < n:
        out.append((t0, min(bs, n - t0)))
        t0 += bs
    return out


def build_program(cfg):
    D, W, NOWN, NCTX = cfg.D, cfg.W, cfg.NOWN, cfg.NCTX
    KC, NCH, NH, PG, LT, NOC, KC2 = cfg.KC, cfg.NCH, cfg.NH, cfg.PG, cfg.LT, cfg.NOC, cfg.KC2
    NT_OWN = NOWN // 128
    ND = D // 512 if D >= 512 else 1
    DB = min(D, 512)
    nc = bass.Bass("TRN2", target_bir_lowering=False)

    def din(name, shape, dt=F32):
        return nc.dram_tensor(name, shape, dt, kind="ExternalInput").ap()

    def dscr(name, shape, dt=F32):
        return nc.dram_tensor(name, shape, dt, kind="Internal").ap()

    x_own_d = din("x_own", [NOWN, D])
    x_oc_d = din("x_oc", [NOC, D])
    dpar_d = din("dpar", [128, KC * 3])
    chp_d = din("chp", [128, NCH * 12])
    plp_d = din("plp", [128, NCH * 2])
    pt_d = din("ptm", [128, 4 * 128])
    ident_d = din("ident", [128, 128])
    w_ada_d = din("w_ada", [D, 3 * D])
    b_ada_d = din("b_ada", [1, 3 * D])
    w_in_d = din("w_in", [D, 4 * W])
    w_r_d = din("w_r2", [2 * NH * 256, 256])
    w_i_d = din("w_i2", [2 * NH * 256, 256])
    w_pool_d = din("w_pool", [4 * PG, PG])
    w_out_d = din("w_out", [2 * W, D])
    g_final_d = din("g_final", [1, D])
    out_d = nc.dram_tensor("out", [NOWN, D], F32, kind="ExternalOutput").ap()

    XA_d = dscr("s_xa", [W, NOWN])
    XAOC_d = dscr("s_xaoc", [W, NOC])
    SG_d = dscr("s_sg", [2 * W, NOWN], BF16)
    Z_d = dscr("s_z", [W, NOWN], BF16)
    MIX_d = dscr("s_mix", [2 * W, NOWN], BF16)
    WOB_d = dscr("s_wob", [2 * W, D], BF16)
    XNEW_d = dscr("s_xnew", [NOWN, D])
    GATE_d = dscr("s_gate", [128, D])

    w_in_r = w_in_d.rearrange("(kc p) c -> p kc c", p=128)
    w_ada_r = w_ada_d.rearrange("(kc p) c -> p kc c", p=128)

    with (
        nc.sbuf_tensor("arena", [128, NA], F32) as ar,
        nc.psum_tensor("ps", [128, 8, 512], F32) as ps,
        nc.Block() as block,
    ):
        P = Prog(nc)
        A = Arena(ar)
        psb = [ps[:, b, :].bitcast(BF16) for b in range(8)]

        identb = A.bf16(128)
        chp = A.f32(NCH * 12); chp3 = r3(chp, 12)
        cl = A.f32(NCH * 4); cl3 = r3(cl, 4)
        plp = A.f32(NCH * 2); plp3 = r3(plp, 2)
        bps = A.f32(NCH)
        hb = A.f32(NCH * 4); hb3 = r3(hb, 4)
        dpar = A.f32(KC * 3); dpar3 = r3(dpar, 3)
        modT = A.f32(2 * KC * 2); modT3 = r3(modT, 2)
        gmod = A.f32(KC * 2); gmod3 = r3(gmod, 2)
        PTb = A.bf16(4 * 128); PTb3 = r3(PTb, 128)
        NHC = (NOWN + NOC) // 128
        ssqH = A.f32(NHC); rstdH = A.f32(NHC)
        ssqD = A.f32(NT_OWN * ND); ssqD3 = r3(ssqD, ND)
        ssqE = A.f32(NT_OWN); rstdE = A.f32(NT_OWN)
        BASE = A.off

        P.stage = "S0"
        t_dpar = P.dma("sp", "l0", dmaf(dpar, dpar_d[:, :]))
        t_chp = P.dma("sp", "l1", dmaf(chp, chp_d[:, :]))
        t_plp = P.dma("sp", "l2", dmaf(plp, plp_d[:, :]))
        ptf = A.f32(512)
        identf = A.f32(128)
        t_pt = P.dma("sp", "l3", dmaf(ptf, pt_d[:, :]))
        t_id = P.dma("sp", "l4", dmaf(identf, ident_d[:, :]))
        c2s = A.bf16(KC * 2); c2s3 = r3(c2s, 2)
        tmpE = A.f32(NCH * 2); tmpE3 = r3(tmpE, 2)
        ones = A.f32(128)
        modrows = A.f32(3 * D)
        bada = A.f32(3 * D)
        wb0 = [A.bf16(KC * 512) for _ in range(2)]
        wb03 = [r3(w, 512) for w in wb0]
        gst = [A.f32(512) for _ in range(2)]

        t_bada = [P.dma("sp", "l5", dmaf(bada[0:1, :], b_ada_d[:, :])),
                  P.dma("sp", "l6", dmaf(bada[1:2, :], b_ada_d[:, :]))]
        t_zero = [P.op("dve", mset(ssqH, 0.0)), P.op("dve", mset(ssqD, 0.0)),
                  P.op("dve", mset(ones, 1.0))]
        t_idb = P.op("dve", cp(identb, identf), [t_id])
        t_ptb = P.op("dve", cp(PTb, ptf), [t_pt])
        t_c2s = P.op("act", act(c2s3, dpar3[:, :, 1:3], AF.Silu), [t_dpar])
        t_e = P.op("act", act(tmpE3, chp3[:, :, 6:8], AF.Exp, scale=-1.0), [t_chp])
        t_spl = P.op("act", act(tmpE, tmpE, AF.Ln, bias=1.0), [t_e])
        t_cl = [P.op("dve", ts1(cl3[:, :, 0:2], tmpE3, -8.0, ALU.mult), [t_spl]),
                P.op("dve", ts1(cl3[:, :, 2:4], tmpE3, -4.0, ALU.mult), [t_spl])]
        t_hb = P.op("dve", ts1(hb3, chp3[:, :, 8:12], 0.5, ALU.mult), [t_chp])
        t_half = P.op("dve", ts1(chp3[:, :, 0:6], chp3[:, :, 0:6], 0.5, ALU.mult), [t_chp])
        t_bps = P.op("dve", tt(bps, plp3[:, :, 0], plp3[:, :, 1], ALU.mult), [t_plp])

        wfree = [None, None]
        bfree = [None, None]
        evs = []
        for cb, (c0, cn) in enumerate(blocks_of(3 * D)):
            sl = cb % 2
            t_w = P.dma("pool", "w%d" % sl, dmaf(wb03[sl][:, :, 0:cn], w_ada_r[:, :, c0:c0 + cn]), [wfree[sl]])
            t = None
            for kc in range(KC):
                t = P.op("pe", mm(ps[0:2, sl, 0:cn], c2s3[:, kc, :], wb03[sl][:, kc, 0:cn], kc == 0, kc == KC - 1),
                         [t_w, t_c2s, bfree[sl]] if kc == 0 else [])
            wfree[sl] = t
            t_ev = P.op("dve", tt(modrows[0:2, c0:c0 + cn], ps[0:2, sl, 0:cn],
                                  bada[0:2, c0:c0 + cn], ALU.add), [t, t_bada])
            bfree[sl] = t_ev
            evs.append(t_ev)
        t_p1 = P.op("dve", ts1(modrows[0:2, D:2 * D], modrows[0:2, D:2 * D], 1.0, ALU.add), [evs])
        NJ = 2 * D // 128
        t = None
        for j in range(NJ):
            t = P.op("pe", mm(ps[:, 2, j * 2:(j + 1) * 2], modrows[0:2, j * 128:(j + 1) * 128], identf[0:2, 0:2], True, True),
                     [t_p1, t_id] if j == 0 else [])
        t_modT = P.op("dve", cp(modT, ps[:, 2, 0:NJ * 2]), [t])
        t_gm = [P.op("dve", tt(gmod3[:, :, r], dpar3[:, :, 0], modT3[:, KC:2 * KC, r], ALU.mult), [t_modT, t_dpar])
                for r in range(2)]
        gfree = [None, None]
        gbank = [None, None]
        for n in range(D // DB):
            sl = n % 2
            t = P.op("pe", mm(ps[:, 3 + sl, 0:DB], ones[0:1, 0:128], modrows[0:1, 2 * D + n * DB:2 * D + (n + 1) * DB], True, True),
                     [evs, t_zero, gbank[sl]])
            t_c = P.op("act", act(gst[sl][:, 0:DB], ps[:, 3 + sl, 0:DB], AF.Identity), [t, gfree[sl]])
            gbank[sl] = t_c
            gfree[sl] = P.dma("sp", "gs%d" % sl, dmaf(GATE_d[:, n * DB:(n + 1) * DB], gst[sl][:, 0:DB]), [t_c])

        P.checkpoint("S0", getattr(cfg, "stop", None))
        P.barrier()
        P.nobar.add("wc")
        t_wc = None
        for i in range(2 * W // 128):
            t_wc = P.dma("pool", "wc", dmaf(WOB_d[i * 128:(i + 1) * 128, :], w_out_d[i * 128:(i + 1) * 128, :]))

        hcol = [0]

        def stage_A(x_d, ntok, tile_mod, own):
            P.stage = "A-H-own%d" % own
            P.barrier()
            A.off = BASE
            hT = A.bf16(KC * ntok); hT3 = r3(hT, ntok)
            mark = A.off
            xt = [A.f32(D) for _ in range(2)]
            xn = A.bf16(D)
            NT = ntok // 128
            xt_free = [None, None]
            xn_free = None
            bank_free = [None] * 4
            bk = 0
            G4 = min(4, KC)
            for tix in range(NT):
                s = tix % 2
                col = hcol[0]; hcol[0] += 1
                t_ld = P.dma("sp", "xt%d" % s, dmaf(xt[s], x_d[tix * 128:(tix + 1) * 128, :]), [xt_free[s]])
                P.checkpoint("H0", getattr(cfg, "stop", None))
                t_sq = P.op("act", act(xn, xt[s], AF.Square, accum=ssqH[:, col:col + 1]), [t_ld, xn_free, t_zero])
                P.checkpoint("H1", getattr(cfg, "stop", None))
                t_r1 = P.op("dve", ts(rstdH[:, col:col + 1], ssqH[:, col:col + 1], 1.0 / D, EPS, ALU.mult, ALU.add), [t_sq])
                t_r2 = P.op("act", act(rstdH[:, col:col + 1], rstdH[:, col:col + 1], AF.Sqrt), [t_r1])
                t_r3 = P.op("dve", recip(rstdH[:, col:col + 1], rstdH[:, col:col + 1]), [t_r2])
                P.checkpoint("H2", getattr(cfg, "stop", None))
                t_n = P.op("dve", ts1(xn, xt[s], rstdH[:, col:col + 1], ALU.mult), [t_r3, t_sq])
                P.checkpoint("H3", getattr(cfg, "stop", None))
                xt_free[s] = t_n
                r = tile_mod[tix]
                for k4 in range(KC // G4):
                    b = bk % 4; bk += 1
                    t_tr = None
                    for q in range(G4):
                        kc = k4 * G4 + q
                        t_tr = P.op("pe", tr(psb[b][:, q * 128:(q + 1) * 128], xn[:, kc * 128:(kc + 1) * 128], identb),
                                    [t_n, t_idb, bank_free[b]] if q == 0 else [])
                    P.checkpoint("H4", getattr(cfg, "stop", None))
                    evl = []
                    for q in range(G4):
                        kc = k4 * G4 + q
                        o = hT3[:, kc, tix * 128:(tix + 1) * 128]
                        i_ = psb[b][:, q * 128:(q + 1) * 128]
                        if q == 1:
                            P.checkpoint("H5", getattr(cfg, "stop", None))
                        if True:
                            evl.append(P.op("act", act(o, i_, AF.Identity, bias=modT3[:, kc, r:r + 1], scale=gmod3[:, kc, r:r + 1]),
                                            [t_tr, t_gm, t_modT]))
                        else:
                            evl.append(P.op("dve", ts(o, i_, gmod3[:, kc, r:r + 1], modT3[:, kc, r:r + 1], ALU.mult, ALU.add),
                                            [t_tr, t_gm, t_modT]))
                    bank_free[b] = evl
                    xn_free = t_tr
                    P.checkpoint("H6", getattr(cfg, "stop", None))
                P.checkpoint("H7", getattr(cfg, "stop", None))
            P.checkpoint("A1h" if own else "A2h", getattr(cfg, "stop", None))
            P.barrier()
            P.stage = "A-G-own%d" % own
            A.off = mark
            wbw = A.off
            wb = [A.bf16(KC * 256) for _ in range(2)]
            wb3 = [r3(w, 256) for w in wb]
            wb2 = ar[:, wbw:wbw + KC * 256].bitcast(BF16)
            wb23 = r3(wb2, 512)
            st = [A.f32(512) for _ in range(4)]
            stb = [s_.bitcast(BF16) for s_ in st]
            blks = blocks_of(ntok)
            chunks = []
            for c in range(W // 256):
                chunks.append((c * 256, "xa", XA_d if own else XAOC_d, c * 256))
            if own:
                for c in range(2 * W // 256):
                    chunks.append((2 * W + c * 256, "sg", SG_d, c * 256))
            wfree = [None, None]
            bank_free = [None] * 8
            st_free = [None] * 4
            bk = 0; sk = 0; ek = 0
            for ci, (c0, kind, dst, row0) in enumerate(chunks):
                sl = ci % 2
                t_w = P.dma("pool", "w%d" % sl, dmaf(wb3[sl], w_in_r[:, :, c0:c0 + 256]), [wfree[sl]])
                t_last = None
                for sub in range(2):
                    for (t0, n) in blks:
                        b = bk % 8; bk += 1
                        t_mm = None
                        for kc in range(KC):
                            t_mm = P.op("pe", mm(ps[:, b, 0:n], wb3[sl][:, kc, sub * 128:(sub + 1) * 128], hT3[:, kc, t0:t0 + n],
                                                 kc == 0, kc == KC - 1), [t_w, bank_free[b]] if kc == 0 else [])
                        t_last = t_mm
                        ss = sk % 4; sk += 1
                        rows = slice(row0 + sub * 128, row0 + (sub + 1) * 128)
                        if kind == "xa":
                            if ek % 2 == 0:
                                t_ev = P.op("act", act(st[ss][:, 0:n], ps[:, b, 0:n], AF.Identity), [t_mm, st_free[ss]])
                            else:
                                t_ev = P.op("dve", cp(st[ss][:, 0:n], ps[:, b, 0:n]), [t_mm, st_free[ss]])
                            ek += 1
                            st_free[ss] = P.dma("sp", "st%d" % ss, dmaf(dst[rows, t0:t0 + n], st[ss][:, 0:n]), [t_ev])
                        else:
                            t_ev = P.op("act", act(stb[ss][:, 0:n], ps[:, b, 0:n], AF.Silu), [t_mm, st_free[ss]])
                            st_free[ss] = P.dma("sp", "st%d" % ss, dmaf(dst[rows, t0:t0 + n], stb[ss][:, 0:n]), [t_ev])
                        bank_free[b] = t_ev
                wfree[sl] = t_last
            if not own:
                return
            P.checkpoint("A1g", getattr(cfg, "stop", None))
            P.stage = "A-XB"
            CGW = 256
            NJ2 = CGW // 128
            xbT = [A.bf16(CGW) for _ in range(2)]
            zst = [A.bf16(512) for _ in range(4)]
            xbT_free = [None, None]
            zst_free = [None] * 4
            banka_free = [bank_free[0], bank_free[1]]
            bankb_free = [[bank_free[2 + j]] for j in range(4)]
            pre = [x for x in bank_free if x is not None]
            ek = 0
            zk = 0
            for cg in range(W // CGW):
                c0 = W + cg * CGW
                sl = cg % 2
                bset = (cg % 2) * NJ2
                t_w = P.dma("pool", "w%d" % sl, dmaf(wb3[sl], w_in_r[:, :, c0:c0 + CGW]), [wfree[sl]])
                t_last = None
                pm_last = [None] * NJ2
                evq = {}

                def do_pool(tix):
                    nonlocal t_last, zk
                    xs = tix % 2
                    t_ev = evq.pop(tix)
                    t_pm = None
                    q4 = tix % 4
                    for j in range(NJ2):
                        g = (cg * CGW + j * 128) // PG
                        bb = 2 + bset + j
                        t_pm = P.op("pe", mm(ps[:, bb, q4 * 128:(q4 + 1) * 128], xbT[xs][:, j * 128:(j + 1) * 128], PTb3[:, g, :], True, True),
                                    [t_ev, t_ptb] + (bankb_free[bset + j] if q4 == 0 else []))
                        pm_last[j] = t_pm
                    xbT_free[xs] = t_pm
                    t_last = t_pm
                    if q4 == 3:
                        for j in range(NJ2):
                            bb = 2 + bset + j
                            zs = zk % 4; zk += 1
                            if zs % 2 == 0:
                                t_z = P.op("act", act(zst[zs], ps[:, bb, :], AF.Identity), [pm_last[j], zst_free[zs]])
                            else:
                                t_z = P.op("dve", cp(zst[zs], ps[:, bb, :]), [pm_last[j], zst_free[zs]])
                            bankb_free[bset + j] = [t_z]
                            zst_free[zs] = P.dma("sp", "zs%d" % zs,
                                                 dmaf(Z_d[cg * CGW + j * 128:cg * CGW + (j + 1) * 128, (tix // 4) * 512:(tix // 4 + 1) * 512], zst[zs]),
                                                 [t_z])

                for tix in range(NT):
                    ba = tix % 2
                    t_mm = None
                    for kc in range(KC):
                        t_mm = P.op("pe", mm(ps[:, ba, 0:CGW], hT3[:, kc, tix * 128:(tix + 1) * 128], wb3[sl][:, kc, :], kc == 0, kc == KC - 1),
                                    [t_w, banka_free[ba], pre] if kc == 0 else [])
                    xs = tix % 2
                    eng = "act" if ek % 2 == 0 else "dve"
                    ek += 1
                    if eng == "act":
                        t_ev = P.op("act", act(xbT[xs], ps[:, ba, 0:CGW], AF.Identity), [t_mm, xbT_free[xs]])
                    else:
                        t_ev = P.op("dve", cp(xbT[xs], ps[:, ba, 0:CGW]), [t_mm, xbT_free[xs]])
                    banka_free[ba] = t_ev
                    evq[tix] = t_ev
                    if tix >= 1:
                        do_pool(tix - 1)
                do_pool(NT - 1)
                wfree[sl] = t_last

        P.checkpoint("WC", getattr(cfg, "stop", None))
        stage_A(x_own_d, NOWN, [0] * NT_OWN, True)
        P.checkpoint("A1", getattr(cfg, "stop", None))
        stage_A(x_oc_d, NOC, [1] * (NCTX // 128) + [0] * (NOWN // 128), False)
        P.checkpoint("A2", getattr(cfg, "stop", None))

        P.stage = "B"
        P.barrier()
        A.off = BASE
        XL = A.f32(2 * (LT + 4)); XL3 = r3(XL, LT + 4)
        XC = A.f32(2 * (NCTX + 4)); XC3 = r3(XC, NCTX + 4)
        U = A.f32(2 * LT); U3 = r3(U, LT)
        UC = A.f32(2 * NCTX); UC3 = r3(UC, NCTX)
        UB = A.bf16(2 * LT); UB3 = r3(UB, LT)
        UCB = A.bf16(2 * NCTX); UCB3 = r3(UCB, NCTX)
        SG = A.bf16(2 * NOWN); SG3 = r3(SG, NOWN)
        HR = A.f32(2 * NOWN); HR3 = r3(HR, NOWN)
        WG = [[A.bf16(2 * 256) for _ in range(2)] for _ in range(2)]
        WG3 = [[r3(w, 256) for w in ws] for ws in WG]
        TMP = {}
        for nm in ("r", "i", "a", "m", "b", "h"):
            TMP[nm] = [[A.f32(512) for _ in range(2)] for _ in range(2)]
        MST = [[A.bf16(512) for _ in range(2)] for _ in range(2)]
        t_hz = [P.op("pool", mset(XL3[:, :, 0:2], 0.0)), P.op("pool", mset(XL3[:, :, LT + 2:LT + 4], 0.0)),
                P.op("pool", mset(XC3[:, :, 0:2], 0.0)), P.op("pool", mset(XC3[:, :, NCTX + 2:NCTX + 4], 0.0))]
        mst_free = [[None, None], [None, None]]
        mk = [0, 0]
        hfree = [[None, None], [None, None]]
        def ld_x(hd, deps):
            out = []
            for j in range(2):
                rows = slice(hd * 256 + j * 128, hd * 256 + (j + 1) * 128)
                out.append(P.dma("sp", "b0", dmaf(XL3[:, j, 2:2 + NOWN], XA_d[rows, :]), deps))
                out.append(P.dma("sp", "b1", dmaf(XL3[:, j, 2 + NOWN:2 + LT], XAOC_d[rows, NCTX:NOC]), deps))
                out.append(P.dma("sp", "b2", dmaf(XC3[:, j, 2:2 + NCTX], XAOC_d[rows, 0:NCTX]), deps))
            return out

        def ld_sg(hd, deps):
            out = []
            for j in range(2):
                rows = slice(hd * 256 + j * 128, hd * 256 + (j + 1) * 128)
                out.append(P.dma("sp", "b3", dmaf(SG3[:, j, :], SG_d[rows, :]), deps))
            return out

        def ld_wg(hd, deps):
            out = []
            for gi, wd in enumerate((w_r_d, w_i_d)):
                for d in range(2):
                    r0 = (d * NH + hd) * 256
                    out.append(P.dma("pool", "b4", dmaf(WG3[gi][d], wd[r0:r0 + 256, :].rearrange("(k p) c -> p k c", p=128)), deps))
            return out

        nx_x = ld_x(0, [])
        nx_sg = ld_sg(0, [])
        nx_wg = ld_wg(0, [])
        slot_free = [[None, None], [None, None]]
        pset_free = [[None], [None]]
        prev_um = []; prev_gate = []; prev_ty = []
        for hd in range(NH):
            ldx = nx_x; ld = nx_sg; t_wg = nx_wg
            cur_um = []; cur_gate = []; cur_ty = []; cur_mx = []
            t_cv = []
            conv_t = {}

            def do_conv(blk):
                src, t0, n, _ = blk
                X3, Uo, Ub = (XC3, UC3, UCB3) if src == "C" else (XL3, U3, UB3)
                fl = []; cl_ = []
                for j in range(2):
                    ch = hd * 2 + j
                    t = P.op("pool", ts(Uo[:, j, t0:t0 + n], X3[:, j, t0:t0 + n], chp3[:, ch, 0:1], chp3[:, ch, 5:6], ALU.mult, ALU.add),
                             [ldx, t_hz, t_half, prev_um])
                    t_cv.append(t)
                    for k in range(1, 5):
                        t = P.op("dve", stt(Uo[:, j, t0:t0 + n], X3[:, j, t0 + k:t0 + k + n], chp3[:, ch, k:k + 1], Uo[:, j, t0:t0 + n],
                                            ALU.mult, ALU.add), [t])
                    t_cv.append(t)
                    t2 = P.op("pool", ts1(Ub[:, j, t0:t0 + n], Uo[:, j, t0:t0 + n], 2.0, ALU.mult), [t, prev_gate])
                    fl.append(t); cl_.append(t2)
                conv_t[(src, t0)] = (fl, cl_)

            sR = [("C", 0, NCTX, False)]
            for (t0, n) in reversed(blocks_of(LT)):
                sR.append(("L", t0, n, t0 < NOWN))
            sF = [("C", 0, NCTX, False)]
            for (t0, n) in blocks_of(NOWN):
                sF.append(("L", t0, n, True))
            pairsR = [sR[i:i + 2] for i in range(0, len(sR), 2)]
            pairsF = [sF[i:i + 2] for i in range(0, len(sF), 2)]
            work = [(1, True, pr, pi) for pi, pr in enumerate(pairsR)] + [(0, False, pr, None) for pr in pairsF]
            for blk in pairsR[0]:
                do_conv(blk)
            hr_task = {}
            bi = 0
            prev = [None, None]
            last_d = None
            for (d, rev, pair, pi) in work:
                if d != last_d:
                    prev = [None, None]
                    last_d = d
                units = []
                for (src, t0, n, is_own) in pair:
                    ub = UCB3 if src == "C" else UB3
                    uf = UC3 if src == "C" else U3
                    fl, cl_ = conv_t[(src, t0)]
                    pset = bi % 2
                    slot = bi % 2
                    bi += 1
                    mmt = {}
                    first = True
                    for gi in range(2):
                        for jo in range(2):
                            b = pset * 4 + gi * 2 + jo
                            t = None
                            for ji in range(2):
                                t = P.op("pe", mm(ps[:, b, 0:n], WG3[gi][d][:, ji, jo * 128:(jo + 1) * 128], ub[:, ji, t0:t0 + n], ji == 0, ji == 1),
                                         ([t_wg, cl_, pset_free[pset]] if first else []))
                                first = False
                            mmt[(gi, jo)] = t
                            cur_gate.append(t)
                    for jo in range(2):
                        units.append((src, t0, n, is_own, uf, pset, slot, jo, mmt, fl[jo]))
                acts = []
                rd = {}
                for (src, t0, n, is_own, uf, pset, slot, jo, mmt, t_uf) in units:
                    ch = hd * 2 + jo
                    R_ = TMP["r"][jo][slot][:, 0:n]; I_ = TMP["i"][jo][slot][:, 0:n]
                    A_ = TMP["a"][jo][slot][:, 0:n]; M_ = TMP["m"][jo][slot][:, 0:n]
                    t_r = P.op("act", act(R_, ps[:, pset * 4 + jo, 0:n], AF.Tanh, bias=hb3[:, ch, d:d + 1], scale=0.5),
                               [mmt[(0, jo)], slot_free[jo][slot], t_hb])
                    t_i = P.op("act", act(I_, ps[:, pset * 4 + 2 + jo, 0:n], AF.Tanh, bias=hb3[:, ch, 2 + d:3 + d], scale=0.5),
                               [mmt[(1, jo)]])
                    rd.setdefault(pset, []).extend([t_r, t_i])
                    t_a = P.op("act", act(A_, R_, AF.Exp, bias=cl3[:, ch, 2 + d:3 + d], scale=cl3[:, ch, 2 + d:3 + d]), [t_r, t_cl])
                    t_a2 = P.op("act", act(M_, R_, AF.Exp, bias=cl3[:, ch, d:d + 1], scale=cl3[:, ch, d:d + 1]), [t_r])
                    acts.append((t_i, t_a, t_a2))
                for k_, v_ in rd.items():
                    pset_free[k_] = v_
                tms = []
                for ui, u_ in enumerate(units):
                    (src, t0, n, is_own, uf, pset, slot, jo, mmt, t_uf) = u_
                    M_ = TMP["m"][jo][slot][:, 0:n]
                    tms.append(P.op("act", act(M_, M_, AF.Sqrt, bias=1.0, scale=-1.0), [acts[ui][2]]))
                if pi is not None and pi + 1 < len(pairsR):
                    for blk in pairsR[pi + 1]:
                        do_conv(blk)
                    if pi + 2 == len(pairsR) and hd + 1 < NH:
                        nx_x = ld_x(hd + 1, [t_cv])
                for ui, u_ in enumerate(units):
                    (src, t0, n, is_own, uf, pset, slot, jo, mmt, t_uf) = u_
                    ch = hd * 2 + jo
                    t_i, t_a, t_a2 = acts[ui]
                    t_m = tms[ui]
                    R_ = TMP["r"][jo][slot][:, 0:n]; I_ = TMP["i"][jo][slot][:, 0:n]
                    A_ = TMP["a"][jo][slot][:, 0:n]; M_ = TMP["m"][jo][slot][:, 0:n]
                    B_ = TMP["b"][jo][slot][:, 0:n]
                    t_um = P.op("pool", tt(M_, M_, uf[:, jo, t0:t0 + n], ALU.mult), [t_m, t_uf])
                    cur_um.append(t_um)
                    t_b2 = P.op("dve", stt(B_, I_, 1.0, M_, ALU.add, ALU.mult), [t_i, t_um])
                    if d == 1 and is_own:
                        H_ = HR3[:, jo, t0:t0 + n]
                    else:
                        H_ = TMP["h"][jo][slot][:, 0:n]
                    if prev[jo] is None:
                        init = 0.0; pt_ = None
                    else:
                        init, pt_ = prev[jo]
                    if rev:
                        t_s = P.op("dve", scan(H_[:, ::-1], A_[:, ::-1], B_[:, ::-1], init),
                                   [t_a, t_b2, pt_, hfree[jo][slot], prev_ty if (d == 1 and is_own) else None])
                        prev[jo] = (H_[:, 0:1], t_s)
                    else:
                        t_s = P.op("dve", scan(H_, A_, B_, init), [t_a, t_b2, pt_, hfree[jo][slot]])
                        prev[jo] = (H_[:, n - 1:n], t_s)
                    slot_free[jo][slot] = [t_s]
                    if d == 1 and is_own:
                        hr_task[(jo, t0)] = t_s
                    if d == 0 and is_own:
                        msl = mk[jo] % 2; mk[jo] += 1
                        t_y = P.op("pool", tt(R_, H_, HR3[:, jo, t0:t0 + n], ALU.add), [t_s, hr_task[(jo, t0)]])
                        t_mx = P.op("pool", tt(MST[jo][msl][:, 0:n], R_, SG3[:, jo, t0:t0 + n], ALU.mult), [t_y, ld, mst_free[jo][msl]])
                        mst_free[jo][msl] = P.dma("sp", "m%d%d" % (jo, msl),
                                                  dmaf(MIX_d[ch * 128:(ch + 1) * 128, t0:t0 + n], MST[jo][msl][:, 0:n]), [t_mx])
                        slot_free[jo][slot] = [t_s, t_mx]
                        hfree[jo][slot] = t_y
                        cur_ty.append(t_y); cur_mx.append(t_mx)
            if hd + 1 < NH:
                nx_sg = ld_sg(hd + 1, [cur_mx])
                nx_wg = ld_wg(hd + 1, [cur_gate[-1]])
            prev_um = cur_um; prev_gate = [cur_gate[-1]]; prev_ty = cur_ty

        P.checkpoint("B", getattr(cfg, "stop", None))
        P.stage = "C"
        P.barrier()
        A.off = BASE
        KI = PG // 128
        Zg = A.bf16(KI * NOWN); Zg3 = r3(Zg, NOWN)
        wp = [A.bf16(KI * 128) for _ in range(2)]; wp3 = [r3(w, 128) for w in wp]
        sgb = [A.bf16(NOWN) for _ in range(2)]
        ctmp = [A.f32(512) for _ in range(2)]
        cst = [A.bf16(512) for _ in range(2)]
        wp_free = [None, None]; sgb_free = [None, None]; ctmp_free = [None, None]; cst_free = [None, None]
        cbank_free = [None] * 8
        zg_free = []
        ck = 0; jk = 0
        for g in range(4):
            t_z = P.dma("sp", "c0", dmaf(Zg3, Z_d[g * PG:(g + 1) * PG, :].rearrange("(k p) t -> p k t", p=128)), [zg_free])
            zg_free = []
            for jo in range(KI):
                sl = jk % 2; jk += 1
                ch = (g * PG) // 128 + jo
                t_w = P.dma("pool", "c1%d" % sl,
                            dmaf(wp3[sl], w_pool_d[g * PG:(g + 1) * PG, jo * 128:(jo + 1) * 128].rearrange("(k p) c -> p k c", p=128)),
                            [wp_free[sl]])
                t_sg = P.dma("sp", "c2%d" % sl, dmaf(sgb[sl], SG_d[W + ch * 128:W + (ch + 1) * 128, :]), [sgb_free[sl]])
                t_mm = None
                t_cm = None
                for (t0, n) in blocks_of(NOWN):
                    b = ck % 8
                    cs = ck % 2
                    ck += 1
                    for ki in range(KI):
                        t_mm = P.op("pe", mm(ps[:, b, 0:n], wp3[sl][:, ki, :], Zg3[:, ki, t0:t0 + n], ki == 0, ki == KI - 1),
                                    [t_w, t_z, cbank_free[b]] if ki == 0 else [])
                    t_ce = P.op("act", act(ctmp[cs][:, 0:n], ps[:, b, 0:n], AF.Identity, bias=bps[:, ch:ch + 1], scale=plp3[:, ch, 1:2]),
                                [t_mm, ctmp_free[cs], t_bps])
                    cbank_free[b] = t_ce
                    t_cm = P.op("dve", tt(cst[cs][:, 0:n], ctmp[cs][:, 0:n], sgb[sl][:, t0:t0 + n], ALU.mult), [t_ce, t_sg, cst_free[cs]])
                    ctmp_free[cs] = t_cm
                    cst_free[cs] = P.dma("sp", "c3%d" % cs, dmaf(MIX_d[W + ch * 128:W + (ch + 1) * 128, t0:t0 + n], cst[cs][:, 0:n]), [t_cm])
                wp_free[sl] = t_mm
                sgb_free[sl] = t_cm
                zg_free.append(t_mm)

        P.checkpoint("C", getattr(cfg, "stop", None))
        P.stage = "D"
        P.barrier()
        A.off = BASE
        TB = 512
        QK = min(16, KC2)
        NQ = KC2 // QK
        MB = A.bf16(KC2 * TB); MB3 = r3(MB, TB)
        wo = [A.bf16(QK * DB) for _ in range(4)]; wo3 = [r3(w, DB) for w in wo]
        gate = A.f32(D)
        xch = [A.f32(4 * DB) for _ in range(2)]; xch3 = [r3(x_, DB) for x_ in xch]
        xnw = [A.f32(4 * DB) for _ in range(2)]; xnw3 = [r3(x_, DB) for x_ in xnw]
        dtmp = [A.f32(DB) for _ in range(2)]
        djunk = A.bf16(DB)
        t_gate = P.dma("sp", "d0", dmaf(gate, GATE_d[:, :]))
        x_own_r = x_own_d.rearrange("(t p) d -> p t d", p=128)
        xnew_r = XNEW_d.rearrange("(t p) d -> p t d", p=128)
        wo_free = [None] * 4
        xch_free = [None, None]; xnw_free = [None, None]; dtmp_free = [None, None]
        dbank_free = [None] * 8
        mb_free = []
        wk = 0; dk = 0; tk = 0
        prev_q = None
        for tb in range(NOWN // TB):
            t_mb = P.dma("sp", "d1", dmaf(MB3, MIX_d.rearrange("(k p) t -> p k t", p=128)[:, :, tb * TB:(tb + 1) * TB]), [mb_free])
            mb_free = []
            for dg in range(D // DB):
                xs = dk % 2
                bset = (dk % 2) * 4
                dk += 1
                t_x = P.dma("sp", "d2%d" % xs, dmaf(xch3[xs], x_own_r[:, tb * 4:(tb + 1) * 4, dg * DB:(dg + 1) * DB]), [xch_free[xs]])
                t_mm = [None] * 4
                for q in range(NQ):
                    ws = wk % 4; wk += 1
                    t_w = P.dma("sp", "d3%d" % ws,
                                dmaf(wo3[ws], WOB_d[q * QK * 128:(q + 1) * QK * 128, dg * DB:(dg + 1) * DB].rearrange("(k p) c -> p k c", p=128)),
                                [wo_free[ws], t_wc])
                    for t4 in range(4):
                        for kc in range(QK):
                            kk = q * QK + kc
                            t_mm[t4] = P.op("pe", mm(ps[:, bset + t4, 0:DB], MB3[:, kk, t4 * 128:(t4 + 1) * 128], wo3[ws][:, kc, :],
                                                     kk == 0, kk == KC2 - 1),
                                            ([t_w, t_mb] + ([dbank_free[bset + t4]] if q == 0 else [])) if kc == 0 else [])
                    wo_free[ws] = t_mm[3]
                for t4 in range(4):
                    ds_ = tk % 2; tk += 1
                    tile_g = tb * 4 + t4
                    t_g = P.op("dve", tt(dtmp[ds_], ps[:, bset + t4, 0:DB], gate[:, dg * DB:(dg + 1) * DB], ALU.mult),
                               [t_mm[t4], t_gate, dtmp_free[ds_]])
                    dbank_free[bset + t4] = t_g
                    t_a = P.op("pool", tt(xnw3[xs][:, t4, :], dtmp[ds_], xch3[xs][:, t4, :], ALU.add),
                               [t_g, t_x, xnw_free[xs] if t4 == 0 else None])
                    dtmp_free[ds_] = t_a
                    t_q = P.op("act", act(djunk, xnw3[xs][:, t4, :], AF.Square, accum=ssqD3[:, tile_g, dg:dg + 1]), [t_a, t_zero, prev_q])
                    prev_q = t_q
                    last_a = t_a; last_q = t_q
                xch_free[xs] = last_a
                xnw_free[xs] = [P.dma("pool", "d4%d" % xs, dmaf(xnew_r[:, tb * 4:(tb + 1) * 4, dg * DB:(dg + 1) * DB], xnw3[xs]), [last_a]), last_q]
                mb_free.append(t_mm[3])

        P.checkpoint("D", getattr(cfg, "stop", None))
        P.stage = "E"
        P.barrier()
        A.off = BASE
        gfb = A.f32(D)
        et = [A.f32(D) for _ in range(2)]
        eo = [A.f32(D) for _ in range(2)]
        t_gf = P.dma("sp", "e0", dmaf(gfb, g_final_d.partition_broadcast(128)))
        t_s1 = P.op("dve", lambda e: e.reduce_sum(ssqE, ssqD3, axis=mybir.AxisListType.X))
        t_s2 = P.op("dve", ts(rstdE, ssqE, 1.0 / D, EPS, ALU.mult, ALU.add), [t_s1])
        t_s3 = P.op("act", act(rstdE, rstdE, AF.Sqrt), [t_s2])
        t_s4 = P.op("dve", recip(rstdE, rstdE), [t_s3])
        et_free = [None, None]; eo_free = [None, None]
        outs = []
        for tix in range(NT_OWN):
            s = tix % 2
            t_l = P.dma("sp", "e1%d" % s, dmaf(et[s], XNEW_d[tix * 128:(tix + 1) * 128, :]), [et_free[s]])
            t_o = P.op("dve", stt(eo[s], et[s], rstdE[:, tix:tix + 1], gfb, ALU.mult, ALU.mult), [t_l, t_gf, t_s4, eo_free[s]])
            et_free[s] = t_o
            eo_free[s] = P.dma("sp", "e2%d" % s, dmaf(out_d[tix * 128:(tix + 1) * 128, :], eo[s]), [t_o])
            outs.append(eo_free[s])
        P.barrier(final=True)

        P.finalize()

        @block.sync
        def _(e):
            P.emit("sp", e)

        @block.scalar
        def _(e):
            P.emit("act", e)

        @block.vector
        def _(e):
            P.emit("dve", e)

        @block.gpsimd
        def _(e):
            P.emit("pool", e)

        @block.tensor
        def _(e):
            P.emit("pe", e)

    nc._dbg_names = P.names
    return nc


def _pool_matrix(w):
    L = GRID_W
    left = w // 2
    right = w - 1 - left
    M = np.zeros((L, L), np.float64)
    for t in range(L):
        lo = max(t - left, 0)
        hi = min(t + right, L - 1) + 1
        M[t, lo:hi] = 1.0 / (hi - lo)
    M -= np.eye(L)
    big = np.zeros((128, 128), np.float64)
    big[:64, :64] = M
    big[64:, 64:] = M
    return big.T.astype(np.float32)


def _colmajor(v, n):
    return np.ascontiguousarray(v.reshape(n, 128, -1).transpose(1, 0, 2))


def make_core_inputs(cfg, inp, b, s):
    D, W, NOWN, NCTX, KC, NCH, NH = cfg.D, cfg.W, cfg.NOWN, cfg.NCTX, cfg.KC, cfg.NCH, cfg.NH
    f = np.float32
    x = inp["x"][b]
    ctx = inp["ctx"][b]
    if s == 0:
        x_own = x[:NOWN]
        x_oth = x[NOWN:]
        cx = ctx
    else:
        x_own = x[NOWN:][::-1]
        x_oth = x[:NOWN][::-1]
        cx = ctx[::-1]
    x_oc = np.concatenate([cx, x_oth], axis=0)
    dF, dR = (0, 1) if s == 0 else (1, 0)
    cw = inp["conv_w"][0]
    conv5 = np.zeros((W, 5), f)
    if s == 0:
        conv5[:, 1:5] = cw.T
    else:
        conv5[:, 0:4] = cw[::-1].T
    chp = np.zeros((W, 12), f)
    chp[:, 0:5] = conv5
    chp[:, 5] = inp["conv_b"][0]
    chp[:, 6] = inp["lru_lambda"][0, dF]
    chp[:, 7] = inp["lru_lambda"][0, dR]
    chp[:, 8] = inp["b_rgate"][0, dF]
    chp[:, 9] = inp["b_rgate"][0, dR]
    chp[:, 10] = inp["b_igate"][0, dF]
    chp[:, 11] = inp["b_igate"][0, dR]
    plp = np.stack([inp["b_pool"][0], inp["pool_scale"][0]], axis=1).astype(f)
    dpar = np.stack([inp["g_norm"][0], inp["c"][b], inp["c_ctx"]], axis=1).astype(f)
    pts = []
    for w in POOL_WINDOWS:
        m = _pool_matrix(w)
        if s == 1:
            m = m[::-1, ::-1]
        pts.append(m)
    ptm = np.ascontiguousarray(np.stack(pts, axis=1)).reshape(128, 4 * 128)
    w_r = inp["w_rgate"][0]
    w_i = inp["w_igate"][0]
    w_r2 = np.stack([w_r[dF], w_r[dR]], axis=0).reshape(2 * NH * 256, 256)
    w_i2 = np.stack([w_i[dF], w_i[dR]], axis=0).reshape(2 * NH * 256, 256)
    return {
        "x_own": np.ascontiguousarray(x_own, dtype=f),
        "x_oc": np.ascontiguousarray(x_oc, dtype=f),
        "dpar": _colmajor(dpar, KC).reshape(128, KC * 3),
        "chp": _colmajor(chp, NCH).reshape(128, NCH * 12),
        "plp": _colmajor(plp, NCH).reshape(128, NCH * 2),
        "ptm": ptm.astype(f),
        "ident": np.eye(128, dtype=f),
        "w_ada": np.ascontiguousarray(inp["w_ada"][0], dtype=f),
        "b_ada": np.ascontiguousarray(inp["b_ada"][0].reshape(1, -1), dtype=f),
        "w_in": np.ascontiguousarray(inp["w_in"][0], dtype=f),
        "w_r2": np.ascontiguousarray(w_r2, dtype=f),
        "w_i2": np.ascontiguousarray(w_i2, dtype=f),
        "w_pool": np.ascontiguousarray(inp["w_pool"][0].reshape(-1, cfg.PG), dtype=f),
        "w_out": np.ascontiguousarray(inp["w_out"][0], dtype=f),
        "g_final": np.ascontiguousarray(inp["g_final"].reshape(1, -1), dtype=f),
    }


def run_cfg(cfg, inp, n_batch):
    inp = {k: np.asarray(v) for k, v in inp.items()}
    nc = build_program(cfg)
    in_maps = []
    for b in range(n_batch):
        for s in range(2):
            in_maps.append(make_core_inputs(cfg, inp, b, s))
    res = run_bass_kernel_spmd(nc, in_maps, core_ids=list(range(2 * n_batch)))
    out = np.zeros((n_batch, 2 * cfg.NOWN, cfg.D), np.float32)
    for b in range(n_batch):
        o0 = res.results[2 * b]["out"]
        o1 = res.results[2 * b + 1]["out"]
        out[b, :cfg.NOWN] = o0
        out[b, cfg.NOWN:] = o1[::-1]
    return out


def kernel(**inputs):
    cfg = Cfg()
    return run_cfg(cfg, inputs, 4)
```

```python
import numpy as np
import concourse.bass as bass
import concourse.mybir as mybir
from concourse.bass_utils import run_bass_kernel_spmd

F32 = mybir.dt.float32
BF16 = mybir.dt.bfloat16
AF = mybir.ActivationFunctionType
ALU = mybir.AluOpType
EPS = 1e-6
NA = 50000
POOL_WINDOWS = (2, 4, 8, 16)
GRID_W = 64


class Cfg:
    def __init__(s, D=4096, W=4096, NOWN=2048, NCTX=256):
        s.D = D; s.W = W; s.NOWN = NOWN; s.NCTX = NCTX
        s.KC = D // 128; s.NCH = W // 128; s.NH = W // 256; s.PG = W // 4
        s.LT = 2 * NOWN; s.NOC = NCTX + NOWN; s.KC2 = 2 * W // 128


class Task:
    __slots__ = ("eng", "fn", "deps", "lane", "sem", "val", "needs_inc", "tag")


class Prog:
    ENG = ("pe", "act", "dve", "pool", "sp")
    CENG = ("pe", "act", "dve", "pool")

    def __init__(s, nc):
        s.nc = nc
        s.tasks = {e: [] for e in s.ENG}
        s.lane_cnt = {}
        s.lane_sem = {}
        s.last_dma = {}
        s.last_comp = {}
        s.engsem = {e: nc.alloc_semaphore(name="es_" + e) for e in s.CENG}
        s.stage = ""
        s.names = {}
        s.dead = False
        s.nobar = set()

    def _add(s, eng, fn, deps):
        t = Task()
        t.eng = eng; t.fn = fn; t.lane = None; t.sem = None; t.val = None; t.needs_inc = False
        dl = []
        for d in deps:
            if d is None:
                continue
            if isinstance(d, (list, tuple)):
                for x in d:
                    if x is None:
                        continue
                    if isinstance(x, (list, tuple)):
                        dl.extend(y for y in x if y is not None)
                    else:
                        dl.append(x)
            else:
                dl.append(d)
        t.deps = dl
        t.tag = s.stage
        if not s.dead:
            s.tasks[eng].append(t)
        return t

    def op(s, eng, fn, deps=()):
        t = s._add(eng, fn, deps)
        if fn is not None and not s.dead:
            s.last_comp[eng] = t
        return t

    def dma(s, q, lane, fn, deps=()):
        t = s._add(q, fn, deps)
        t.lane = lane
        if s.dead:
            return t
        if lane not in s.lane_sem:
            s.lane_sem[lane] = s.nc.alloc_semaphore(name="ls_" + lane)
            s.lane_cnt[lane] = 0
        s.lane_cnt[lane] += 1
        t.sem = s.lane_sem[lane]
        t.val = 16 * s.lane_cnt[lane]
        s.last_dma[lane] = t
        return t

    def checkpoint(s, name, stop):
        if stop == name:
            s.dead = True

    def barrier(s, final=False):
        if final:
            s.dead = False
        deps = [s.last_comp[e] for e in s.CENG if e in s.last_comp] + [v for k, v in s.last_dma.items() if final or k not in s.nobar]
        for e in s.ENG:
            s.op(e, None, deps)

    def finalize(s):
        for e in s.ENG:
            for t in s.tasks[e]:
                for d in t.deps:
                    if d.lane is None:
                        d.needs_inc = True
        for e in s.CENG:
            n = 0
            for t in s.tasks[e]:
                if t.lane is None and t.fn is not None and t.needs_inc:
                    n += 1
                    t.sem = s.engsem[e]
                    t.val = n

    def emit(s, name, e):
        waited = {}
        for t in s.tasks[name]:
            need = {}
            for d in t.deps:
                k = id(d.sem)
                if k not in need or need[k][1] < d.val:
                    need[k] = (d.sem, d.val)
            for k, (sm, v) in need.items():
                if waited.get(k, 0) < v:
                    e.wait_ge(sm, v)
                    waited[k] = v
            if t.fn is None:
                continue
            ins = t.fn(e)
            try:
                s.names[str(ins.ins.name)] = (name, t.tag)
            except Exception:
                pass
            if t.lane is not None:
                ins.then_inc(t.sem, 16)
            elif t.needs_inc:
                ins.then_inc(t.sem, 1)


def mm(out, lhsT, rhs, start, stop):
    return lambda e: e.matmul(out, lhsT, rhs, start=start, stop=stop)


def tr(out, in_, ident):
    return lambda e: e.transpose(out, in_, ident)


def act(out, in_, func, bias=None, scale=None, accum=None):
    def f(e):
        kw = {}
        if bias is not None:
            kw["bias"] = bias
        if scale is not None:
            kw["scale"] = scale
        if accum is not None:
            kw["accum_out"] = accum
        return e.activation(out, in_, func, **kw)
    return f


def dmaf(out, in_):
    return lambda e: e.dma_start(out=out, in_=in_)


def tt(out, a, b, op):
    return lambda e: e.tensor_tensor(out, a, b, op)


def ts(out, a, s1, s2, op0, op1):
    return lambda e: e.tensor_scalar(out, a, s1, s2, op0, op1)


def ts1(out, a, s1, op):
    return lambda e: e.tensor_single_scalar(out, a, s1, op)


def stt(out, in0, scalar, in1, op0, op1):
    return lambda e: e.scalar_tensor_tensor(out, in0, scalar, in1, op0, op1)


def cp(out, in_):
    return lambda e: e.tensor_copy(out, in_)


def scan(out, d0, d1, init):
    return lambda e: e.tensor_tensor_scan(out, d0, d1, init, ALU.mult, ALU.add)


def mset(ap, v):
    return lambda e: e.memset(ap, v)


def recip(out, in_):
    return lambda e: e.reciprocal(out, in_)


class Arena:
    def __init__(s, ar):
        s.ar = ar; s.off = 0

    def f32(s, n):
        s.off = (s.off + 15) // 16 * 16
        v = s.ar[:, s.off:s.off + n]
        s.off += n
        assert s.off <= NA, ("arena overflow", s.off)
        return v

    def bf16(s, n):
        w = (n + 1) // 2
        s.off = (s.off + 15) // 16 * 16
        v = s.ar[:, s.off:s.off + w].bitcast(BF16)
        s.off += w
        assert s.off <= NA, ("arena overflow", s.off)
        return v


def r3(ap, b):
    return ap.rearrange("p (a b) -> p a b", b=b)


def blocks_of(n, bs=512):
    out = []
    t0 = 0
    while t0 < n:
        out.append((t0, min(bs, n - t0)))
        t0 += bs
    return out


def build_program(cfg):
    D, W, NOWN, NCTX = cfg.D, cfg.W, cfg.NOWN, cfg.NCTX
    KC, NCH, NH, PG, LT, NOC, KC2 = cfg.KC, cfg.NCH, cfg.NH, cfg.PG, cfg.LT, cfg.NOC, cfg.KC2
    NT_OWN = NOWN // 128
    ND = D // 512 if D >= 512 else 1
    DB = min(D, 512)
    nc = bass.Bass("TRN2", target_bir_lowering=False)

    def din(name, shape, dt=F32):
        return nc.dram_tensor(name, shape, dt, kind="ExternalInput").ap()

    def dscr(name, shape, dt=F32):
        return nc.dram_tensor(name, shape, dt, kind="Internal").ap()

    x_own_d = din("x_own", [NOWN, D])
    x_oc_d = din("x_oc", [NOC, D])
    dpar_d = din("dpar", [128, KC * 3])
    chp_d = din("chp", [128, NCH * 12])
    plp_d = din("plp", [128, NCH * 2])
    pt_d = din("ptm", [128, 4 * 128])
    ident_d = din("ident", [128, 128])
    w_ada_d = din("w_ada", [D, 3 * D])
    b_ada_d = din("b_ada", [1, 3 * D])
    w_in_d = din("w_in", [D, 4 * W])
    w_r_d = din("w_r2", [2 * NH * 256, 256])
    w_i_d = din("w_i2", [2 * NH * 256, 256])
    w_pool_d = din("w_pool", [4 * PG, PG])
    w_out_d = din("w_out", [2 * W, D])
    g_final_d = din("g_final", [1, D])
    out_d = nc.dram_tensor("out", [NOWN, D], F32, kind="ExternalOutput").ap()

    XA_d = dscr("s_xa", [W, NOWN])
    XAOC_d = dscr("s_xaoc", [W, NOC])
    SG_d = dscr("s_sg", [2 * W, NOWN], BF16)
    Z_d = dscr("s_z", [W, NOWN], BF16)
    MIX_d = dscr("s_mix", [2 * W, NOWN], BF16)
    WOB_d = dscr("s_wob", [2 * W, D], BF16)
    XNEW_d = dscr("s_xnew", [NOWN, D])
    GATE_d = dscr("s_gate", [128, D])

    w_in_r = w_in_d.rearrange("(kc p) c -> p kc c", p=128)
    w_ada_r = w_ada_d.rearrange("(kc p) c -> p kc c", p=128)

    with (
        nc.sbuf_tensor("arena", [128, NA], F32) as ar,
        nc.psum_tensor("ps", [128, 8, 512], F32) as ps,
        nc.Block() as block,
    ):
        P = Prog(nc)
        A = Arena(ar)
        psb = [ps[:, b, :].bitcast(BF16) for b in range(8)]

        identb = A.bf16(128)
        chp = A.f32(NCH * 12); chp3 = r3(chp, 12)
        cl = A.f32(NCH * 4); cl3 = r3(cl, 4)
        plp = A.f32(NCH * 2); plp3 = r3(plp, 2)
        bps = A.f32(NCH)
        hb = A.f32(NCH * 4); hb3 = r3(hb, 4)
        dpar = A.f32(KC * 3); dpar3 = r3(dpar, 3)
        modT = A.f32(2 * KC * 2); modT3 = r3(modT, 2)
        gmod = A.f32(KC * 2); gmod3 = r3(gmod, 2)
        PTb = A.bf16(4 * 128); PTb3 = r3(PTb, 128)
        NHC = (NOWN + NOC) // 128
        ssqH = A.f32(NHC); rstdH = A.f32(NHC)
        ssqD = A.f32(NT_OWN * ND); ssqD3 = r3(ssqD, ND)
        ssqE = A.f32(NT_OWN); rstdE = A.f32(NT_OWN)
        BASE = A.off

        P.stage = "S0"
        t_dpar = P.dma("sp", "l0", dmaf(dpar, dpar_d[:, :]))
        t_chp = P.dma("sp", "l1", dmaf(chp, chp_d[:, :]))
        t_plp = P.dma("sp", "l2", dmaf(plp, plp_d[:, :]))
        ptf = A.f32(512)
        identf = A.f32(128)
        t_pt = P.dma("sp", "l3", dmaf(ptf, pt_d[:, :]))
        t_id = P.dma("sp", "l4", dmaf(identf, ident_d[:, :]))
        c2s = A.bf16(KC * 2); c2s3 = r3(c2s, 2)
        tmpE = A.f32(NCH * 2); tmpE3 = r3(tmpE, 2)
        ones = A.f32(128)
        modrows = A.f32(3 * D)
        bada = A.f32(3 * D)
        wb0 = [A.bf16(KC * 512) for _ in range(2)]
        wb03 = [r3(w, 512) for w in wb0]
        gst = [A.f32(512) for _ in range(2)]

        t_bada = [P.dma("sp", "l5", dmaf(bada[0:1, :], b_ada_d[:, :])),
                  P.dma("sp", "l6", dmaf(bada[1:2, :], b_ada_d[:, :]))]
        t_zero = [P.op("dve", mset(ssqH, 0.0)), P.op("dve", mset(ssqD, 0.0)),
                  P.op("dve", mset(ones, 1.0))]
        t_idb = P.op("dve", cp(identb, identf), [t_id])
        t_ptb = P.op("dve", cp(PTb, ptf), [t_pt])
        t_c2s = P.op("act", act(c2s3, dpar3[:, :, 1:3], AF.Silu), [t_dpar])
        t_e = P.op("act", act(tmpE3, chp3[:, :, 6:8], AF.Exp, scale=-1.0), [t_chp])
        t_spl = P.op("act", act(tmpE, tmpE, AF.Ln, bias=1.0), [t_e])
        t_cl = [P.op("dve", ts1(cl3[:, :, 0:2], tmpE3, -8.0, ALU.mult), [t_spl]),
                P.op("dve", ts1(cl3[:, :, 2:4], tmpE3, -4.0, ALU.mult), [t_spl])]
        t_hb = P.op("dve", ts1(hb3, chp3[:, :, 8:12], 0.5, ALU.mult), [t_chp])
        t_bps = P.op("dve", tt(bps, plp3[:, :, 0], plp3[:, :, 1], ALU.mult), [t_plp])

        wfree = [None, None]
        bfree = [None, None]
        evs = []
        for cb, (c0, cn) in enumerate(blocks_of(3 * D)):
            sl = cb % 2
            t_w = P.dma("pool", "w%d" % sl, dmaf(wb03[sl][:, :, 0:cn], w_ada_r[:, :, c0:c0 + cn]), [wfree[sl]])
            t = None
            for kc in range(KC):
                t = P.op("pe", mm(ps[0:2, sl, 0:cn], c2s3[:, kc, :], wb03[sl][:, kc, 0:cn], kc == 0, kc == KC - 1),
                         [t_w, t_c2s, bfree[sl]] if kc == 0 else [])
            wfree[sl] = t
            t_ev = P.op("dve", tt(modrows[0:2, c0:c0 + cn], ps[0:2, sl, 0:cn],
                                  bada[0:2, c0:c0 + cn], ALU.add), [t, t_bada])
            bfree[sl] = t_ev
            evs.append(t_ev)
        t_p1 = P.op("dve", ts1(modrows[0:2, D:2 * D], modrows[0:2, D:2 * D], 1.0, ALU.add), [evs])
        NJ = 2 * D // 128
        t = None
        for j in range(NJ):
            t = P.op("pe", mm(ps[:, 2, j * 2:(j + 1) * 2], modrows[0:2, j * 128:(j + 1) * 128], identf[0:2, 0:2], True, True),
                     [t_p1, t_id] if j == 0 else [])
        t_modT = P.op("dve", cp(modT, ps[:, 2, 0:NJ * 2]), [t])
        t_gm = [P.op("dve", tt(gmod3[:, :, r], dpar3[:, :, 0], modT3[:, KC:2 * KC, r], ALU.mult), [t_modT, t_dpar])
                for r in range(2)]
        gfree = [None, None]
        gbank = [None, None]
        for n in range(D // DB):
            sl = n % 2
            t = P.op("pe", mm(ps[:, 3 + sl, 0:DB], ones[0:1, 0:128], modrows[0:1, 2 * D + n * DB:2 * D + (n + 1) * DB], True, True),
                     [evs, t_zero, gbank[sl]])
            t_c = P.op("act", act(gst[sl][:, 0:DB], ps[:, 3 + sl, 0:DB], AF.Identity), [t, gfree[sl]])
            gbank[sl] = t_c
            gfree[sl] = P.dma("sp", "gs%d" % sl, dmaf(GATE_d[:, n * DB:(n + 1) * DB], gst[sl][:, 0:DB]), [t_c])

        P.checkpoint("S0", getattr(cfg, "stop", None))
        P.barrier()
        P.nobar.add("wc")
        t_wc = None
        for i in range(2 * W // 128):
            t_wc = P.dma("pool", "wc", dmaf(WOB_d[i * 128:(i + 1) * 128, :], w_out_d[i * 128:(i + 1) * 128, :]))

        hcol = [0]

        def stage_A(x_d, ntok, tile_mod, own):
            P.stage = "A-H-own%d" % own
            P.barrier()
            A.off = BASE
            hT = A.bf16(KC * ntok); hT3 = r3(hT, ntok)
            mark = A.off
            xt = [A.f32(D) for _ in range(2)]
            xn = A.bf16(D)
            NT = ntok // 128
            xt_free = [None, None]
            xn_free = None
            bank_free = [None] * 4
            bk = 0
            G4 = min(4, KC)
            for tix in range(NT):
                s = tix % 2
                col = hcol[0]; hcol[0] += 1
                t_ld = P.dma("sp", "xt%d" % s, dmaf(xt[s], x_d[tix * 128:(tix + 1) * 128, :]), [xt_free[s]])
                P.checkpoint("H0", getattr(cfg, "stop", None))
                t_sq = P.op("act", act(xn, xt[s], AF.Square, accum=ssqH[:, col:col + 1]), [t_ld, xn_free, t_zero])
                P.checkpoint("H1", getattr(cfg, "stop", None))
                t_r1 = P.op("dve", ts(rstdH[:, col:col + 1], ssqH[:, col:col + 1], 1.0 / D, EPS, ALU.mult, ALU.add), [t_sq])
                t_r2 = P.op("act", act(rstdH[:, col:col + 1], rstdH[:, col:col + 1], AF.Sqrt), [t_r1])
                t_r3 = P.op("dve", recip(rstdH[:, col:col + 1], rstdH[:, col:col + 1]), [t_r2])
                P.checkpoint("H2", getattr(cfg, "stop", None))
                t_n = P.op("dve", ts1(xn, xt[s], rstdH[:, col:col + 1], ALU.mult), [t_r3, t_sq])
                P.checkpoint("H3", getattr(cfg, "stop", None))
                xt_free[s] = t_n
                r = tile_mod[tix]
                for k4 in range(KC // G4):
                    b = bk % 4; bk += 1
                    t_tr = None
                    for q in range(G4):
                        kc = k4 * G4 + q
                        t_tr = P.op("pe", tr(psb[b][:, q * 128:(q + 1) * 128], xn[:, kc * 128:(kc + 1) * 128], identb),
                                    [t_n, t_idb, bank_free[b]] if q == 0 else [])
                    P.checkpoint("H4", getattr(cfg, "stop", None))
                    evl = []
                    for q in range(G4):
                        kc = k4 * G4 + q
                        o = hT3[:, kc, tix * 128:(tix + 1) * 128]
                        i_ = psb[b][:, q * 128:(q + 1) * 128]
                        if q == 1:
                            P.checkpoint("H5", getattr(cfg, "stop", None))
                        if True:
                            evl.append(P.op("act", act(o, i_, AF.Identity, bias=modT3[:, kc, r:r + 1], scale=gmod3[:, kc, r:r + 1]),
                                            [t_tr, t_gm, t_modT]))
                        else:
                            evl.append(P.op("dve", ts(o, i_, gmod3[:, kc, r:r + 1], modT3[:, kc, r:r + 1], ALU.mult, ALU.add),
                                            [t_tr, t_gm, t_modT]))
                    bank_free[b] = evl
                    xn_free = t_tr
                    P.checkpoint("H6", getattr(cfg, "stop", None))
                P.checkpoint("H7", getattr(cfg, "stop", None))
            P.checkpoint("A1h" if own else "A2h", getattr(cfg, "stop", None))
            P.barrier()
            P.stage = "A-G-own%d" % own
            A.off = mark
            wbw = A.off
            wb = [A.bf16(KC * 256) for _ in range(2)]
            wb3 = [r3(w, 256) for w in wb]
            wb2 = ar[:, wbw:wbw + KC * 256].bitcast(BF16)
            wb23 = r3(wb2, 512)
            st = [A.f32(512) for _ in range(4)]
            stb = [s_.bitcast(BF16) for s_ in st]
            blks = blocks_of(ntok)
            chunks = []
            for c in range(W // 256):
                chunks.append((c * 256, "xa", XA_d if own else XAOC_d, c * 256))
            if own:
                for c in range(2 * W // 256):
                    chunks.append((2 * W + c * 256, "sg", SG_d, c * 256))
            wfree = [None, None]
            bank_free = [None] * 8
            st_free = [None] * 4
            bk = 0; sk = 0; ek = 0
            for ci, (c0, kind, dst, row0) in enumerate(chunks):
                sl = ci % 2
                t_w = P.dma("pool", "w%d" % sl, dmaf(wb3[sl], w_in_r[:, :, c0:c0 + 256]), [wfree[sl]])
                t_last = None
                for sub in range(2):
                    for (t0, n) in blks:
                        b = bk % 8; bk += 1
                        t_mm = None
                        for kc in range(KC):
                            t_mm = P.op("pe", mm(ps[:, b, 0:n], wb3[sl][:, kc, sub * 128:(sub + 1) * 128], hT3[:, kc, t0:t0 + n],
                                                 kc == 0, kc == KC - 1), [t_w, bank_free[b]] if kc == 0 else [])
                        t_last = t_mm
                        ss = sk % 4; sk += 1
                        rows = slice(row0 + sub * 128, row0 + (sub + 1) * 128)
                        if kind == "xa":
                            if ek % 2 == 0:
                                t_ev = P.op("act", act(st[ss][:, 0:n], ps[:, b, 0:n], AF.Identity), [t_mm, st_free[ss]])
                            else:
                                t_ev = P.op("dve", cp(st[ss][:, 0:n], ps[:, b, 0:n]), [t_mm, st_free[ss]])
                            ek += 1
                            st_free[ss] = P.dma("sp", "st%d" % ss, dmaf(dst[rows, t0:t0 + n], st[ss][:, 0:n]), [t_ev])
                        else:
                            t_ev = P.op("act", act(stb[ss][:, 0:n], ps[:, b, 0:n], AF.Silu), [t_mm, st_free[ss]])
                            st_free[ss] = P.dma("sp", "st%d" % ss, dmaf(dst[rows, t0:t0 + n], stb[ss][:, 0:n]), [t_ev])
                        bank_free[b] = t_ev
                wfree[sl] = t_last
            if not own:
                return
            P.checkpoint("A1g", getattr(cfg, "stop", None))
            P.stage = "A-XB"
            CGW = 256
            NJ2 = CGW // 128
            xbT = [A.bf16(CGW) for _ in range(2)]
            zst = [A.bf16(512) for _ in range(4)]
            xbT_free = [None, None]
            zst_free = [None] * 4
            banka_free = [bank_free[0], bank_free[1]]
            bankb_free = [[bank_free[2 + j]] for j in range(4)]
            pre = [x for x in bank_free if x is not None]
            ek = 0
            zk = 0
            for cg in range(W // CGW):
                c0 = W + cg * CGW
                sl = cg % 2
                bset = (cg % 2) * NJ2
                t_w = P.dma("pool", "w%d" % sl, dmaf(wb3[sl], w_in_r[:, :, c0:c0 + CGW]), [wfree[sl]])
                t_last = None
                pm_last = [None] * NJ2
                evq = {}

                def do_pool(tix):
                    nonlocal t_last, zk
                    xs = tix % 2
                    t_ev = evq.pop(tix)
                    t_pm = None
                    q4 = tix % 4
                    for j in range(NJ2):
                        g = (cg * CGW + j * 128) // PG
                        bb = 2 + bset + j
                        t_pm = P.op("pe", mm(ps[:, bb, q4 * 128:(q4 + 1) * 128], xbT[xs][:, j * 128:(j + 1) * 128], PTb3[:, g, :], True, True),
                                    [t_ev, t_ptb] + (bankb_free[bset + j] if q4 == 0 else []))
                        pm_last[j] = t_pm
                    xbT_free[xs] = t_pm
                    t_last = t_pm
                    if q4 == 3:
                        for j in range(NJ2):
                            bb = 2 + bset + j
                            zs = zk % 4; zk += 1
                            if zs % 2 == 0:
                                t_z = P.op("act", act(zst[zs], ps[:, bb, :], AF.Identity), [pm_last[j], zst_free[zs]])
                            else:
                                t_z = P.op("dve", cp(zst[zs], ps[:, bb, :]), [pm_last[j], zst_free[zs]])
                            bankb_free[bset + j] = [t_z]
                            zst_free[zs] = P.dma("sp", "zs%d" % zs,
                                                 dmaf(Z_d[cg * CGW + j * 128:cg * CGW + (j + 1) * 128, (tix // 4) * 512:(tix // 4 + 1) * 512], zst[zs]),
                                                 [t_z])

                for tix in range(NT):
                    ba = tix % 2
                    t_mm = None
                    for kc in range(KC):
                        t_mm = P.op("pe", mm(ps[:, ba, 0:CGW], hT3[:, kc, tix * 128:(tix + 1) * 128], wb3[sl][:, kc, :], kc == 0, kc == KC - 1),
                                    [t_w, banka_free[ba], pre] if kc == 0 else [])
                    xs = tix % 2
                    eng = "act" if ek % 2 == 0 else "dve"
                    ek += 1
                    if eng == "act":
                        t_ev = P.op("act", act(xbT[xs], ps[:, ba, 0:CGW], AF.Identity), [t_mm, xbT_free[xs]])
                    else:
                        t_ev = P.op("dve", cp(xbT[xs], ps[:, ba, 0:CGW]), [t_mm, xbT_free[xs]])
                    banka_free[ba] = t_ev
                    evq[tix] = t_ev
                    if tix >= 1:
                        do_pool(tix - 1)
                do_pool(NT - 1)
                wfree[sl] = t_last

        P.checkpoint("WC", getattr(cfg, "stop", None))
        stage_A(x_own_d, NOWN, [0] * NT_OWN, True)
        P.checkpoint("A1", getattr(cfg, "stop", None))
        stage_A(x_oc_d, NOC, [1] * (NCTX // 128) + [0] * (NOWN // 128), False)
        P.checkpoint("A2", getattr(cfg, "stop", None))

        P.stage = "B"
        P.barrier()
        A.off = BASE
        XL = A.f32(2 * (LT + 4)); XL3 = r3(XL, LT + 4)
        XC = A.f32(2 * (NCTX + 4)); XC3 = r3(XC, NCTX + 4)
        U = A.f32(2 * LT); U3 = r3(U, LT)
        UC = A.f32(2 * NCTX); UC3 = r3(UC, NCTX)
        UB = A.bf16(2 * LT); UB3 = r3(UB, LT)
        UCB = A.bf16(2 * NCTX); UCB3 = r3(UCB, NCTX)
        SG = A.bf16(2 * NOWN); SG3 = r3(SG, NOWN)
        HR = A.f32(2 * NOWN); HR3 = r3(HR, NOWN)
        WG = [[A.bf16(2 * 256) for _ in range(2)] for _ in range(2)]
        WG3 = [[r3(w, 256) for w in ws] for ws in WG]
        TMP = {}
        for nm in ("r", "i", "a", "m", "b", "h"):
            TMP[nm] = [[A.f32(512) for _ in range(2)] for _ in range(2)]
        MST = [[A.bf16(512) for _ in range(2)] for _ in range(2)]
        t_hz = [P.op("pool", mset(XL3[:, :, 0:2], 0.0)), P.op("pool", mset(XL3[:, :, LT + 2:LT + 4], 0.0)),
                P.op("pool", mset(XC3[:, :, 0:2], 0.0)), P.op("pool", mset(XC3[:, :, NCTX + 2:NCTX + 4], 0.0))]
        mst_free = [[None, None], [None, None]]
        mk = [0, 0]
        hfree = [[None, None], [None, None]]
        def ld_x(hd, deps):
            out = []
            for j in range(2):
                rows = slice(hd * 256 + j * 128, hd * 256 + (j + 1) * 128)
                out.append(P.dma("sp", "b0", dmaf(XL3[:, j, 2:2 + NOWN], XA_d[rows, :]), deps))
                out.append(P.dma("sp", "b1", dmaf(XL3[:, j, 2 + NOWN:2 + LT], XAOC_d[rows, NCTX:NOC]), deps))
                out.append(P.dma("sp", "b2", dmaf(XC3[:, j, 2:2 + NCTX], XAOC_d[rows, 0:NCTX]), deps))
            return out

        def ld_sg(hd, deps):
            out = []
            for j in range(2):
                rows = slice(hd * 256 + j * 128, hd * 256 + (j + 1) * 128)
                out.append(P.dma("sp", "b3", dmaf(SG3[:, j, :], SG_d[rows, :]), deps))
            return out

        def ld_wg(hd, deps):
            out = []
            for gi, wd in enumerate((w_r_d, w_i_d)):
                for d in range(2):
                    r0 = (d * NH + hd) * 256
                    out.append(P.dma("pool", "b4", dmaf(WG3[gi][d], wd[r0:r0 + 256, :].rearrange("(k p) c -> p k c", p=128)), deps))
            return out

        nx_x = ld_x(0, [])
        nx_sg = ld_sg(0, [])
        nx_wg = ld_wg(0, [])
        slot_free = [[None, None], [None, None]]
        pset_free = [[None], [None]]
        prev_b1 = []; prev_gate = []; prev_ty = []
        for hd in range(NH):
            ldx = nx_x; ld = nx_sg; t_wg = nx_wg
            tu = {"C": [], "H": [], "Lo": []}
            t_cv = []

            def do_conv_chunk(name):
                if name == "C":
                    X3, Uo, Ub, a_, b_ = XC3, UC3, UCB3, 0, NCTX
                elif name == "H":
                    X3, Uo, Ub, a_, b_ = XL3, U3, UB3, NOWN, LT
                else:
                    X3, Uo, Ub, a_, b_ = XL3, U3, UB3, 0, NOWN
                for j in range(2):
                    ch = hd * 2 + j
                    t = P.op("pool", ts(Uo[:, j, a_:b_], X3[:, j, a_:b_], chp3[:, ch, 0:1], chp3[:, ch, 5:6], ALU.mult, ALU.add),
                             [ldx, t_hz, t_chp, prev_b1])
                    t_cv.append(t)
                    for k in range(1, 5):
                        t = P.op("dve", stt(Uo[:, j, a_:b_], X3[:, j, a_ + k:b_ + k], chp3[:, ch, k:k + 1], Uo[:, j, a_:b_], ALU.mult, ALU.add), [t])
                    t_cv.append(t)
                    t2 = P.op("act", act(Ub[:, j, a_:b_], Uo[:, j, a_:b_], AF.Identity), [t, prev_gate])
                    tu[name].append(t)
                    tu[name].append(t2)

            def chunk_of(src, t0):
                return "C" if src == "C" else ("H" if t0 >= NOWN else "Lo")

            do_conv_chunk("C")
            do_conv_chunk("H")
            lo_done = [False]
            cur_b1 = []; cur_gate = []; cur_ty = []; cur_mx = []

            seqs = []
            sR = [("C", 0, NCTX, False)]
            for (t0, n) in reversed(blocks_of(LT)):
                sR.append(("L", t0, n, t0 < NOWN))
            seqs.append((1, True, sR))
            sF = [("C", 0, NCTX, False)]
            for (t0, n) in blocks_of(NOWN):
                sF.append(("L", t0, n, True))
            seqs.append((0, False, sF))
            hr_task = {}
            bi = 0
            for (d, rev, sq) in seqs:
                prev = [None, None]
                for p0 in range(0, len(sq), 2):
                    units = []
                    for (src, t0, n, is_own) in sq[p0:p0 + 2]:
                        ub = UCB3 if src == "C" else UB3
                        uf = UC3 if src == "C" else U3
                        pset = bi % 2
                        slot = bi % 2
                        bi += 1
                        mmt = {}
                        first = True
                        for gi in range(2):
                            for jo in range(2):
                                b = pset * 4 + gi * 2 + jo
                                t = None
                                for ji in range(2):
                                    t = P.op("pe", mm(ps[:, b, 0:n], WG3[gi][d][:, ji, jo * 128:(jo + 1) * 128], ub[:, ji, t0:t0 + n], ji == 0, ji == 1),
                                             ([t_wg, tu[chunk_of(src, t0)], pset_free[pset]] if first else []))
                                    first = False
                                mmt[(gi, jo)] = t
                                cur_gate.append(t)
                        for jo in range(2):
                            units.append((src, t0, n, is_own, uf, pset, slot, jo, mmt))
                    acts = []
                    rd = {}
                    for (src, t0, n, is_own, uf, pset, slot, jo, mmt) in units:
                        ch = hd * 2 + jo
                        R_ = TMP["r"][jo][slot][:, 0:n]; I_ = TMP["i"][jo][slot][:, 0:n]
                        A_ = TMP["a"][jo][slot][:, 0:n]; M_ = TMP["m"][jo][slot][:, 0:n]
                        t_r = P.op("act", act(R_, ps[:, pset * 4 + jo, 0:n], AF.Tanh, bias=hb3[:, ch, d:d + 1], scale=0.5),
                                   [mmt[(0, jo)], slot_free[jo][slot], t_hb])
                        t_i = P.op("act", act(I_, ps[:, pset * 4 + 2 + jo, 0:n], AF.Tanh, bias=hb3[:, ch, 2 + d:3 + d], scale=0.5),
                                   [mmt[(1, jo)]])
                        rd.setdefault(pset, []).extend([t_r, t_i])
                        t_a = P.op("act", act(A_, R_, AF.Exp, bias=cl3[:, ch, 2 + d:3 + d], scale=cl3[:, ch, 2 + d:3 + d]), [t_r, t_cl])
                        t_a2 = P.op("act", act(M_, R_, AF.Exp, bias=cl3[:, ch, d:d + 1], scale=cl3[:, ch, d:d + 1]), [t_r])
                        acts.append((t_i, t_a, t_a2))
                    for k_, v_ in rd.items():
                        pset_free[k_] = v_
                    tms = []
                    for ui, (src, t0, n, is_own, uf, pset, slot, jo, mmt) in enumerate(units):
                        M_ = TMP["m"][jo][slot][:, 0:n]
                        tms.append(P.op("act", act(M_, M_, AF.Sqrt, bias=1.0, scale=-1.0), [acts[ui][2]]))
                    if not lo_done[0]:
                        lo_done[0] = True
                        do_conv_chunk("Lo")
                        if hd + 1 < NH:
                            nx_x = ld_x(hd + 1, [t_cv])
                    for ui, (src, t0, n, is_own, uf, pset, slot, jo, mmt) in enumerate(units):
                        ch = hd * 2 + jo
                        t_i, t_a, t_a2 = acts[ui]
                        t_m = tms[ui]
                        R_ = TMP["r"][jo][slot][:, 0:n]; I_ = TMP["i"][jo][slot][:, 0:n]
                        A_ = TMP["a"][jo][slot][:, 0:n]; M_ = TMP["m"][jo][slot][:, 0:n]
                        B_ = TMP["b"][jo][slot][:, 0:n]
                        t_b1 = P.op("dve", stt(B_, I_, 1.0, uf[:, jo, t0:t0 + n], ALU.add, ALU.mult), [t_i, tu[chunk_of(src, t0)]])
                        cur_b1.append(t_b1)
                        t_b2 = P.op("dve", stt(B_, B_, 0.5, M_, ALU.mult, ALU.mult), [t_b1, t_m])
                        if d == 1 and is_own:
                            H_ = HR3[:, jo, t0:t0 + n]
                        else:
                            H_ = TMP["h"][jo][slot][:, 0:n]
                        if prev[jo] is None:
                            init = 0.0; pt_ = None
                        else:
                            init, pt_ = prev[jo]
                        if rev:
                            t_s = P.op("dve", scan(H_[:, ::-1], A_[:, ::-1], B_[:, ::-1], init),
                                       [t_a, t_b2, pt_, hfree[jo][slot], prev_ty if (d == 1 and is_own) else None])
                            prev[jo] = (H_[:, 0:1], t_s)
                        else:
                            t_s = P.op("dve", scan(H_, A_, B_, init), [t_a, t_b2, pt_, hfree[jo][slot]])
                            prev[jo] = (H_[:, n - 1:n], t_s)
                        slot_free[jo][slot] = [t_s]
                        if d == 1 and is_own:
                            hr_task[(jo, t0)] = t_s
                        if d == 0 and is_own:
                            msl = mk[jo] % 2; mk[jo] += 1
                            t_y = P.op("pool", tt(R_, H_, HR3[:, jo, t0:t0 + n], ALU.add), [t_s, hr_task[(jo, t0)]])
                            t_mx = P.op("pool", tt(MST[jo][msl][:, 0:n], R_, SG3[:, jo, t0:t0 + n], ALU.mult), [t_y, ld, mst_free[jo][msl]])
                            mst_free[jo][msl] = P.dma("sp", "m%d%d" % (jo, msl),
                                                      dmaf(MIX_d[ch * 128:(ch + 1) * 128, t0:t0 + n], MST[jo][msl][:, 0:n]), [t_mx])
                            slot_free[jo][slot] = [t_s, t_mx]
                            hfree[jo][slot] = t_y
                            cur_ty.append(t_y); cur_mx.append(t_mx)

            if hd + 1 < NH:
                nx_sg = ld_sg(hd + 1, [cur_mx])
                nx_wg = ld_wg(hd + 1, [cur_gate[-1]])
            prev_b1 = cur_b1; prev_gate = [cur_gate[-1]]; prev_ty = cur_ty

        P.checkpoint("B", getattr(cfg, "stop", None))
        P.stage = "C"
        P.barrier()
        A.off = BASE
        KI = PG // 128
        Zg = A.bf16(KI * NOWN); Zg3 = r3(Zg, NOWN)
        wp = [A.bf16(KI * 128) for _ in range(2)]; wp3 = [r3(w, 128) for w in wp]
        sgb = [A.bf16(NOWN) for _ in range(2)]
        ctmp = [A.f32(512) for _ in range(2)]
        cst = [A.bf16(512) for _ in range(2)]
        wp_free = [None, None]; sgb_free = [None, None]; ctmp_free = [None, None]; cst_free = [None, None]
        cbank_free = [None] * 8
        zg_free = []
        ck = 0; jk = 0
        for g in range(4):
            t_z = P.dma("sp", "c0", dmaf(Zg3, Z_d[g * PG:(g + 1) * PG, :].rearrange("(k p) t -> p k t", p=128)), [zg_free])
            zg_free = []
            for jo in range(KI):
                sl = jk % 2; jk += 1
                ch = (g * PG) // 128 + jo
                t_w = P.dma("pool", "c1%d" % sl,
                            dmaf(wp3[sl], w_pool_d[g * PG:(g + 1) * PG, jo * 128:(jo + 1) * 128].rearrange("(k p) c -> p k c", p=128)),
                            [wp_free[sl]])
                t_sg = P.dma("sp", "c2%d" % sl, dmaf(sgb[sl], SG_d[W + ch * 128:W + (ch + 1) * 128, :]), [sgb_free[sl]])
                t_mm = None
                t_cm = None
                for (t0, n) in blocks_of(NOWN):
                    b = ck % 8
                    cs = ck % 2
                    ck += 1
                    for ki in range(KI):
                        t_mm = P.op("pe", mm(ps[:, b, 0:n], wp3[sl][:, ki, :], Zg3[:, ki, t0:t0 + n], ki == 0, ki == KI - 1),
                                    [t_w, t_z, cbank_free[b]] if ki == 0 else [])
                    t_ce = P.op("act", act(ctmp[cs][:, 0:n], ps[:, b, 0:n], AF.Identity, bias=bps[:, ch:ch + 1], scale=plp3[:, ch, 1:2]),
                                [t_mm, ctmp_free[cs], t_bps])
                    cbank_free[b] = t_ce
                    t_cm = P.op("dve", tt(cst[cs][:, 0:n], ctmp[cs][:, 0:n], sgb[sl][:, t0:t0 + n], ALU.mult), [t_ce, t_sg, cst_free[cs]])
                    ctmp_free[cs] = t_cm
                    cst_free[cs] = P.dma("sp", "c3%d" % cs, dmaf(MIX_d[W + ch * 128:W + (ch + 1) * 128, t0:t0 + n], cst[cs][:, 0:n]), [t_cm])
                wp_free[sl] = t_mm
                sgb_free[sl] = t_cm
                zg_free.append(t_mm)

        P.checkpoint("C", getattr(cfg, "stop", None))
        P.stage = "D"
        P.barrier()
        A.off = BASE
        TB = 512
        QK = min(16, KC2)
        NQ = KC2 // QK
        MB = A.bf16(KC2 * TB); MB3 = r3(MB, TB)
        wo = [A.bf16(QK * DB) for _ in range(4)]; wo3 = [r3(w, DB) for w in wo]
        gate = A.f32(D)
        xch = [A.f32(4 * DB) for _ in range(2)]; xch3 = [r3(x_, DB) for x_ in xch]
        xnw = [A.f32(4 * DB) for _ in range(2)]; xnw3 = [r3(x_, DB) for x_ in xnw]
        dtmp = [A.f32(DB) for _ in range(2)]
        djunk = A.bf16(DB)
        t_gate = P.dma("sp", "d0", dmaf(gate, GATE_d[:, :]))
        x_own_r = x_own_d.rearrange("(t p) d -> p t d", p=128)
        xnew_r = XNEW_d.rearrange("(t p) d -> p t d", p=128)
        wo_free = [None] * 4
        xch_free = [None, None]; xnw_free = [None, None]; dtmp_free = [None, None]
        dbank_free = [None] * 8
        mb_free = []
        wk = 0; dk = 0; tk = 0
        prev_q = None
        for tb in range(NOWN // TB):
            t_mb = P.dma("sp", "d1", dmaf(MB3, MIX_d.rearrange("(k p) t -> p k t", p=128)[:, :, tb * TB:(tb + 1) * TB]), [mb_free])
            mb_free = []
            for dg in range(D // DB):
                xs = dk % 2
                bset = (dk % 2) * 4
                dk += 1
                t_x = P.dma("sp", "d2%d" % xs, dmaf(xch3[xs], x_own_r[:, tb * 4:(tb + 1) * 4, dg * DB:(dg + 1) * DB]), [xch_free[xs]])
                t_mm = [None] * 4
                for q in range(NQ):
                    ws = wk % 4; wk += 1
                    t_w = P.dma("sp", "d3%d" % ws,
                                dmaf(wo3[ws], WOB_d[q * QK * 128:(q + 1) * QK * 128, dg * DB:(dg + 1) * DB].rearrange("(k p) c -> p k c", p=128)),
                                [wo_free[ws], t_wc])
                    for t4 in range(4):
                        for kc in range(QK):
                            kk = q * QK + kc
                            t_mm[t4] = P.op("pe", mm(ps[:, bset + t4, 0:DB], MB3[:, kk, t4 * 128:(t4 + 1) * 128], wo3[ws][:, kc, :],
                                                     kk == 0, kk == KC2 - 1),
                                            ([t_w, t_mb] + ([dbank_free[bset + t4]] if q == 0 else [])) if kc == 0 else [])
                    wo_free[ws] = t_mm[3]
                for t4 in range(4):
                    ds_ = tk % 2; tk += 1
                    tile_g = tb * 4 + t4
                    t_g = P.op("dve", tt(dtmp[ds_], ps[:, bset + t4, 0:DB], gate[:, dg * DB:(dg + 1) * DB], ALU.mult),
                               [t_mm[t4], t_gate, dtmp_free[ds_]])
                    dbank_free[bset + t4] = t_g
                    t_a = P.op("pool", tt(xnw3[xs][:, t4, :], dtmp[ds_], xch3[xs][:, t4, :], ALU.add),
                               [t_g, t_x, xnw_free[xs] if t4 == 0 else None])
                    dtmp_free[ds_] = t_a
                    t_q = P.op("act", act(djunk, xnw3[xs][:, t4, :], AF.Square, accum=ssqD3[:, tile_g, dg:dg + 1]), [t_a, t_zero, prev_q])
                    prev_q = t_q
                    last_a = t_a; last_q = t_q
                xch_free[xs] = last_a
                xnw_free[xs] = [P.dma("pool", "d4%d" % xs, dmaf(xnew_r[:, tb * 4:(tb + 1) * 4, dg * DB:(dg + 1) * DB], xnw3[xs]), [last_a]), last_q]
                mb_free.append(t_mm[3])

        P.checkpoint("D", getattr(cfg, "stop", None))
        P.stage = "E"
        P.barrier()
        A.off = BASE
        gfb = A.f32(D)
        et = [A.f32(D) for _ in range(2)]
        eo = [A.f32(D) for _ in range(2)]
        t_gf = P.dma("sp", "e0", dmaf(gfb, g_final_d.partition_broadcast(128)))
        t_s1 = P.op("dve", lambda e: e.reduce_sum(ssqE, ssqD3, axis=mybir.AxisListType.X))
        t_s2 = P.op("dve", ts(rstdE, ssqE, 1.0 / D, EPS, ALU.mult, ALU.add), [t_s1])
        t_s3 = P.op("act", act(rstdE, rstdE, AF.Sqrt), [t_s2])
        t_s4 = P.op("dve", recip(rstdE, rstdE), [t_s3])
        et_free = [None, None]; eo_free = [None, None]
        outs = []
        for tix in range(NT_OWN):
            s = tix % 2
            t_l = P.dma("sp", "e1%d" % s, dmaf(et[s], XNEW_d[tix * 128:(tix + 1) * 128, :]), [et_free[s]])
            t_o = P.op("dve", stt(eo[s], et[s], rstdE[:, tix:tix + 1], gfb, ALU.mult, ALU.mult), [t_l, t_gf, t_s4, eo_free[s]])
            et_free[s] = t_o
            eo_free[s] = P.dma("sp", "e2%d" % s, dmaf(out_d[tix * 128:(tix + 1) * 128, :], eo[s]), [t_o])
            outs.append(eo_free[s])
        P.barrier(final=True)

        P.finalize()

        @block.sync
        def _(e):
            P.emit("sp", e)

        @block.scalar
        def _(e):
            P.emit("act", e)

        @block.vector
        def _(e):
            P.emit("dve", e)

        @block.gpsimd
        def _(e):
            P.emit("pool", e)

        @block.tensor
        def _(e):
            P.emit("pe", e)

    nc._dbg_names = P.names
    return nc


def _pool_matrix(w):
    L = GRID_W
    left = w // 2
    right = w - 1 - left
    M = np.zeros((L, L), np.float64)
    for t in range(L):
        lo = max(t - left, 0)
        hi = min(t + right, L - 1) + 1
        M[t, lo:hi] = 1.0 / (hi - lo)
    M -= np.eye(L)
    big = np.zeros((128, 128), np.float64)
    big[:64, :64] = M
    big[64:, 64:] = M
    return big.T.astype(np.float32)


def _colmajor(v, n):
    return np.ascontiguousarray(v.reshape(n, 128, -1).transpose(1, 0, 2))


def make_core_inputs(cfg, inp, b, s):
    D, W, NOWN, NCTX, KC, NCH, NH = cfg.D, cfg.W, cfg.NOWN, cfg.NCTX, cfg.KC, cfg.NCH, cfg.NH
    f = np.float32
    x = inp["x"][b]
    ctx = inp["ctx"][b]
    if s == 0:
        x_own = x[:NOWN]
        x_oth = x[NOWN:]
        cx = ctx
    else:
        x_own = x[NOWN:][::-1]
        x_oth = x[:NOWN][::-1]
        cx = ctx[::-1]
    x_oc = np.concatenate([cx, x_oth], axis=0)
    dF, dR = (0, 1) if s == 0 else (1, 0)
    cw = inp["conv_w"][0]
    conv5 = np.zeros((W, 5), f)
    if s == 0:
        conv5[:, 1:5] = cw.T
    else:
        conv5[:, 0:4] = cw[::-1].T
    chp = np.zeros((W, 12), f)
    chp[:, 0:5] = conv5
    chp[:, 5] = inp["conv_b"][0]
    chp[:, 6] = inp["lru_lambda"][0, dF]
    chp[:, 7] = inp["lru_lambda"][0, dR]
    chp[:, 8] = inp["b_rgate"][0, dF]
    chp[:, 9] = inp["b_rgate"][0, dR]
    chp[:, 10] = inp["b_igate"][0, dF]
    chp[:, 11] = inp["b_igate"][0, dR]
    plp = np.stack([inp["b_pool"][0], inp["pool_scale"][0]], axis=1).astype(f)
    dpar = np.stack([inp["g_norm"][0], inp["c"][b], inp["c_ctx"]], axis=1).astype(f)
    pts = []
    for w in POOL_WINDOWS:
        m = _pool_matrix(w)
        if s == 1:
            m = m[::-1, ::-1]
        pts.append(m)
    ptm = np.ascontiguousarray(np.stack(pts, axis=1)).reshape(128, 4 * 128)
    w_r = inp["w_rgate"][0]
    w_i = inp["w_igate"][0]
    w_r2 = np.stack([w_r[dF], w_r[dR]], axis=0).reshape(2 * NH * 256, 256)
    w_i2 = np.stack([w_i[dF], w_i[dR]], axis=0).reshape(2 * NH * 256, 256)
    return {
        "x_own": np.ascontiguousarray(x_own, dtype=f),
        "x_oc": np.ascontiguousarray(x_oc, dtype=f),
        "dpar": _colmajor(dpar, KC).reshape(128, KC * 3),
        "chp": _colmajor(chp, NCH).reshape(128, NCH * 12),
        "plp": _colmajor(plp, NCH).reshape(128, NCH * 2),
        "ptm": ptm.astype(f),
        "ident": np.eye(128, dtype=f),
        "w_ada": np.ascontiguousarray(inp["w_ada"][0], dtype=f),
        "b_ada": np.ascontiguousarray(inp["b_ada"][0].reshape(1, -1), dtype=f),
        "w_in": np.ascontiguousarray(inp["w_in"][0], dtype=f),
        "w_r2": np.ascontiguousarray(w_r2, dtype=f),
        "w_i2": np.ascontiguousarray(w_i2, dtype=f),
        "w_pool": np.ascontiguousarray(inp["w_pool"][0].reshape(-1, cfg.PG), dtype=f),
        "w_out": np.ascontiguousarray(inp["w_out"][0], dtype=f),
        "g_final": np.ascontiguousarray(inp["g_final"].reshape(1, -1), dtype=f),
    }


def run_cfg(cfg, inp, n_batch):
    inp = {k: np.asarray(v) for k, v in inp.items()}
    nc = build_program(cfg)
    in_maps = []
    for b in range(n_batch):
        for s in range(2):
            in_maps.append(make_core_inputs(cfg, inp, b, s))
    res = run_bass_kernel_spmd(nc, in_maps, core_ids=list(range(2 * n_batch)))
    out = np.zeros((n_batch, 2 * cfg.NOWN, cfg.D), np.float32)
    for b in range(n_batch):
        o0 = res.results[2 * b]["out"]
        o1 = res.results[2 * b + 1]["out"]
        out[b, :cfg.NOWN] = o0
        out[b, cfg.NOWN:] = o1[::-1]
    return out


def kernel(**inputs):
    cfg = Cfg()
    return run_cfg(cfg, inputs, 4)
```
